# Optimizing a Trainium2 kernel written in Bass

```python
import jax, jax.numpy as jnp
from jax import lax
import numpy as np

D_MODEL = 4096
BATCH = 2
SEQ = 4096
DEPTH = 4
DEC_BATCH = 8
DEC_SEQ = 64
PAST_LEN = 1024

CHUNK = 64
MLP_CHUNK = 128
N_MIXERS = 2
N_A = (DEPTH + N_MIXERS - 1) // N_MIXERS
N_B = DEPTH // N_MIXERS
D_A = D_MODEL
G_A = 8
D_B = D_MODEL
CONV_B = 31
D_FF = 2 * D_MODEL
CONV_F = 3
EPS = 1e-6

kernel_name = 'hybrid_gmlp_conformer_convffn_stream_step'


def rms_norm(x, g):
    xf = x.astype(jnp.float32)
    y = xf * lax.rsqrt(jnp.mean(xf * xf, axis=-1, keepdims=True) + EPS)
    return (y * g.astype(jnp.float32)).astype(x.dtype)


def layer_norm(x, g, b):
    xf = x.astype(jnp.float32)
    mu = jnp.mean(xf, axis=-1, keepdims=True)
    var = jnp.mean(jnp.square(xf - mu), axis=-1, keepdims=True)
    y = (xf - mu) * lax.rsqrt(var + EPS)
    return (y * g.astype(jnp.float32) + b.astype(jnp.float32)).astype(x.dtype)


def causal_dwconv(x, hist, w, b):
    xp = jnp.concatenate([hist.astype(x.dtype), x], axis=1)
    y = lax.conv_general_dilated(xp, w[:, None, :].astype(x.dtype), window_strides=(1,), padding='VALID',
                                 dimension_numbers=('NWC', 'WIO', 'NWC'), feature_group_count=x.shape[-1])
    return y + b, xp[:, xp.shape[1] - (w.shape[0] - 1):]


def chunk_causal_mask():
    pos = jnp.arange(MLP_CHUNK)
    return (pos[None, :] // CHUNK) <= (pos[:, None] // CHUNK)


def gmlp_mixer(h, w_in, b_in, ln_g, ln_b, w_s, b_s, w_out):
    n, t, _ = h.shape
    z = jax.nn.gelu(h @ w_in + b_in, approximate=False)
    u, v = jnp.split(z, 2, axis=-1)
    v = layer_norm(v, ln_g, ln_b)
    n_chunks = -(-t // MLP_CHUNK)
    pad = n_chunks * MLP_CHUNK - t
    vp = jnp.pad(v, ((0, 0), (0, pad), (0, 0))).reshape(n, n_chunks, MLP_CHUNK, G_A, D_A // G_A)
    w_m = jnp.where(chunk_causal_mask()[None], w_s, jnp.zeros((), w_s.dtype))
    mixed = jnp.einsum('gij,bcjgd->bcigd', w_m, vp) + jnp.transpose(b_s)[None, None, :, :, None]
    mixed = mixed.reshape(n, n_chunks * MLP_CHUNK, D_A)[:, :t]
    return (u * mixed) @ w_out, v


def conformer_conv(h, hist, w_in, b_in, w_dw, b_dw, ln_g, ln_b, w_out):
    a, gate = jnp.split(h @ w_in + b_in, 2, axis=-1)
    g = a * jax.nn.sigmoid(gate)
    c, new_hist = causal_dwconv(g, hist, w_dw, b_dw)
    c = jax.nn.silu(layer_norm(c, ln_g, ln_b))
    return c @ w_out, new_hist


def conv_ffn(h, hist, w_up, w_dw, b_dw, w_down):
    up = h @ w_up
    c, new_hist = causal_dwconv(up, hist, w_dw, b_dw)
    gate, val = jnp.split(c, 2, axis=-1)
    return (jax.nn.silu(gate) * val) @ w_down, new_hist


def trunk(x, conv_hist, ffn_hist, p):
    conv_states, ffn_states, v_rows = [], [], []
    for i in range(DEPTH):
        j = i // N_MIXERS
        h = rms_norm(x, p['norm_mix_pre'][i])
        if i % N_MIXERS == 0:
            m, v = gmlp_mixer(h, p['a_w_in'][j], p['a_b_in'][j], p['a_ln_g'][j], p['a_ln_b'][j],
                              p['a_w_s'][j], p['a_b_s'][j], p['a_w_out'][j])
            v_rows.append(v)
        else:
            m, st = conformer_conv(h, conv_hist[j], p['b_w_in'][j], p['b_b_in'][j], p['b_w_dw'][j],
                                   p['b_b_dw'][j], p['b_ln_g'][j], p['b_ln_b'][j], p['b_w_out'][j])
            conv_states.append(st)
        x = x + rms_norm(m, p['norm_mix_post'][i])
        h = rms_norm(x, p['norm_ffn_pre'][i])
        f, st = conv_ffn(h, ffn_hist[i], p['f_w_up'][i], p['f_w_dw'][i], p['f_b_dw'][i], p['f_w_down'][i])
        ffn_states.append(st)
        x = x + rms_norm(f, p['norm_ffn_post'][i])
    return x, jnp.stack(conv_states), jnp.stack(ffn_states), jnp.stack(v_rows)


def setup_inputs(seed: int = 0) -> dict:
    key = jax.random.key(seed)
    ks = jax.random.split(key, 28)

    def nrm(k, shape, scale):
        return jax.random.normal(k, shape, jnp.float32) * scale

    return {
        'x_prompt': nrm(ks[0], (BATCH, SEQ, D_MODEL), 1.0),
        'x_sample': nrm(ks[1], (DEC_BATCH, DEC_SEQ, D_MODEL), 1.0),
        'state_conv': nrm(ks[2], (N_B, DEC_BATCH, CONV_B - 1, D_B), 0.5),
        'state_ffn': nrm(ks[3], (DEPTH, DEC_BATCH, CONV_F - 1, 2 * D_FF), 1.0),
        'norm_mix_pre': 1.0 + nrm(ks[4], (DEPTH, D_MODEL), 0.05),
        'norm_mix_post': 1.0 + nrm(ks[5], (DEPTH, D_MODEL), 0.05),
        'norm_ffn_pre': 1.0 + nrm(ks[6], (DEPTH, D_MODEL), 0.05),
        'norm_ffn_post': 1.0 + nrm(ks[7], (DEPTH, D_MODEL), 0.05),
        'a_w_in': nrm(ks[8], (N_A, D_MODEL, 2 * D_A), D_MODEL ** -0.5),
        'a_b_in': nrm(ks[9], (N_A, 2 * D_A), 0.02),
        'a_ln_g': 1.0 + nrm(ks[10], (N_A, D_A), 0.05),
        'a_ln_b': nrm(ks[11], (N_A, D_A), 0.02),
        'a_w_s': nrm(ks[12], (N_A, G_A, MLP_CHUNK, MLP_CHUNK), MLP_CHUNK ** -0.5),
        'a_b_s': 1.0 + nrm(ks[13], (N_A, G_A, MLP_CHUNK), 0.1),
        'a_w_out': nrm(ks[14], (N_A, D_A, D_MODEL), D_A ** -0.5),
        'b_w_in': nrm(ks[15], (N_B, D_MODEL, 2 * D_B), D_MODEL ** -0.5),
        'b_b_in': nrm(ks[16], (N_B, 2 * D_B), 0.02),
        'b_w_dw': nrm(ks[17], (N_B, CONV_B, D_B), CONV_B ** -0.5),
        'b_b_dw': nrm(ks[18], (N_B, D_B), 0.02),
        'b_ln_g': 1.0 + nrm(ks[19], (N_B, D_B), 0.05),
        'b_ln_b': nrm(ks[20], (N_B, D_B), 0.02),
        'b_w_out': nrm(ks[21], (N_B, D_B, D_MODEL), D_B ** -0.5),
        'f_w_up': nrm(ks[22], (DEPTH, D_MODEL, 2 * D_FF), D_MODEL ** -0.5),
        'f_w_dw': nrm(ks[23], (DEPTH, CONV_F, 2 * D_FF), CONV_F ** -0.5),
        'f_b_dw': nrm(ks[24], (DEPTH, 2 * D_FF), 0.02),
        'f_w_down': nrm(ks[25], (DEPTH, D_FF, D_MODEL), D_FF ** -0.5),
    }


def reference(x_prompt, x_sample, state_conv, state_ffn,
              norm_mix_pre, norm_mix_post, norm_ffn_pre, norm_ffn_post,
              a_w_in, a_b_in, a_ln_g, a_ln_b, a_w_s, a_b_s, a_w_out,
              b_w_in, b_b_in, b_w_dw, b_b_dw, b_ln_g, b_ln_b, b_w_out,
              f_w_up, f_w_dw, f_b_dw, f_w_down):
    p = dict(norm_mix_pre=norm_mix_pre, norm_mix_post=norm_mix_post,
             norm_ffn_pre=norm_ffn_pre, norm_ffn_post=norm_ffn_post,
             a_w_in=a_w_in, a_b_in=a_b_in, a_ln_g=a_ln_g, a_ln_b=a_ln_b,
             a_w_s=a_w_s, a_b_s=a_b_s, a_w_out=a_w_out,
             b_w_in=b_w_in, b_b_in=b_b_in, b_w_dw=b_w_dw, b_b_dw=b_b_dw,
             b_ln_g=b_ln_g, b_ln_b=b_ln_b, b_w_out=b_w_out,
             f_w_up=f_w_up, f_w_dw=f_w_dw, f_b_dw=f_b_dw, f_w_down=f_w_down)
    bsz = x_prompt.shape[0]
    conv_hist0 = jnp.zeros((N_B, bsz, CONV_B - 1, D_B), x_prompt.dtype)
    ffn_hist0 = jnp.zeros((DEPTH, bsz, CONV_F - 1, 2 * D_FF), x_prompt.dtype)
    y_prompt, new_conv_prompt, new_ffn_prompt, _v_prompt = trunk(x_prompt, conv_hist0, ffn_hist0, p)
    y_sample, new_conv_sample, new_ffn_sample, new_v_sample = trunk(x_sample, state_conv, state_ffn, p)
    return (y_prompt, y_sample, new_conv_prompt, new_ffn_prompt, new_conv_sample, new_ffn_sample, new_v_sample)
```

```python
import numpy as np
from contextlib import ExitStack
import concourse.bass as bass
import concourse.mybir as mybir
from concourse.bass_utils import run_bass_kernel_spmd

F32 = mybir.dt.float32
BF16 = mybir.dt.bfloat16
AF = mybir.ActivationFunctionType
ALU = mybir.AluOpType
EPS = 1e-6
NCORES = 8
HALO = 256
SHARD = 1024
NS = 64
CB = 31
CF = 3
GA = 8
TMAX = 384
NW = 4
EPOCH = 8192
ENG = ("pe", "act", "dve", "pool", "sp")


class Cfg:
    def __init__(self, D=4096, depth=4):
        self.D = D
        self.KD = D // 128
        self.KF = 4 * self.KD
        self.KH = 2 * self.KD
        self.depth = depth
        self.NA = (depth + 1) // 2
        self.NB = depth // 2
        self.cpg = max(1, self.KD // GA)
        KD, KF = self.KD, self.KF
        o = 0
        self.o_mpre = o; o += KD
        self.o_mpost = o; o += KD
        self.o_fpre = o; o += KD
        self.o_fpost = o; o += KD
        self.o_fw = o; o += KF * CF
        self.o_fb = o; o += KF
        self.o_mix = o
        self.ppw = o + max(4 * KD, 2 * KD + CB * KD + 3 * KD)
        self.units_per_layer = 3 * KD + KF + 2 * KD
        self.NU = depth * self.units_per_layer
        self.tiles = [
            dict(T=320, segs=[("p", 0, 256, 0, None), ("s", 256, 64, 0, 0)]),
            dict(T=384, segs=[("p", 0, 384, 256, 0)]),
            dict(T=384, segs=[("p", 0, 384, 640, 384)]),
            dict(T=256, segs=[("p", 0, 256, 1024, 768)]),
        ]


class Prog:
    def __init__(self):
        self.ops = {e: [] for e in ENG}
        self.cnt = {e: 0 for e in ENG}
        self.seen = {e: {} for e in ENG}
        self.lastw = {}
        self.readers = {}
        self.dcnt = {}

    def _deps(self, eng, reads, writes):
        raw, other = set(), set()
        for r in reads:
            t = self.lastw.get(r)
            if t is not None:
                raw.add(t)
            if r == "ptb" or (isinstance(r, tuple) and r[0] in ("pm", "pst", "ptr")):
                rd = self.readers.get(r)
                if rd:
                    other.update(rd.values())
        for w in writes:
            t = self.lastw.get(w)
            if t is not None:
                other.add(t)
            rd = self.readers.get(w)
            if rd:
                other.update(rd.values())
        waits = {}
        seen = self.seen[eng]
        for t in raw | other:
            if t[0] == "e":
                src, seq = t[1], t[2]
                if src == eng and (eng == "pe" or t not in raw):
                    continue
                key = src
            else:
                key, seq = ("d", t[1]), t[2]
            if seen.get(key, -1) >= seq:
                continue
            if waits.get(key, -1) < seq:
                waits[key] = seq
        for k, v in waits.items():
            seen[k] = v
        return list(waits.items())

    def _register(self, tok, reads, writes):
        src = tok[1]
        for r in reads:
            self.readers.setdefault(r, {})[src] = tok
        for w in writes:
            self.lastw[w] = tok
            self.readers[w] = {}

    stop = False
    stop_at = None

    def mark(self, name):
        if self.stop_at is not None and name == self.stop_at:
            self.stop = True

    def op(self, eng, fn, reads=(), writes=()):
        if self.stop:
            return
        waits = self._deps(eng, reads, writes)
        seq = self.cnt[eng]
        self.cnt[eng] += 1
        self.ops[eng].append((waits, fn, None))
        self._register(("e", eng, seq), reads, writes)

    def dma(self, eng, fn, key, reads=(), writes=()):
        if self.stop:
            return
        waits = self._deps(eng, reads, writes)
        n = self.dcnt.get(key, 0) + 1
        self.dcnt[key] = n
        self.ops[eng].append((waits, fn, key))
        self._register(("d", key, n), reads, writes)


def build(cfg):
    D, KD, KF, KH = cfg.D, cfg.KD, cfg.KF, cfg.KH
    depth, NA, NB, cpg = cfg.depth, cfg.NA, cfg.NB, cfg.cpg
    HB, HF = CB - 1, CF - 1
    nc = bass.Bass("TRN2", target_bir_lowering=False)

    def din(name, shape):
        return nc.dram_tensor(name, shape, F32, kind="ExternalInput").ap()

    def dout(name, shape):
        return nc.dram_tensor(name, shape, F32, kind="ExternalOutput").ap()

    xp = din("xp", [HALO + SHARD, D])
    xs = din("xs", [NS, D])
    valid_d = din("valid", [128, 1])
    ident_d = din("ident", [128, 128])
    sconv = din("sconv", [NB * HB, D])
    sffn = din("sffn", [depth * HF, 4 * D])
    ws = din("ws", [cfg.NU * 128, KD * 128])
    pp_d = din("pp", [depth * 128, cfg.ppw])
    wst_d = din("wst", [NA * 128, GA * 128])
    bs_d = din("bs", [NA, GA * 128])
    upc = max(1, (240 << 20) // (128 * KD * 128 * 2))
    wcs = [nc.dram_tensor(f"wcache{i}", [min(upc, cfg.NU - i * upc) * 128, KD * 128], BF16, kind="Internal").ap()
           for i in range((cfg.NU + upc - 1) // upc)]
    yp = dout("yp", [SHARD, D])
    ys = dout("ys", [NS, D])
    ncp = dout("ncp", [NB * HB, D])
    nfp = dout("nfp", [depth * HF, 4 * D])
    ncs = dout("ncs", [NB * HB, D])
    nfs = dout("nfs", [depth * HF, 4 * D])
    nvs = dout("nvs", [NA * NS, D])

    P = Prog()
    P.stop_at = (getattr(cfg, "dbg", None) or {}).get("stop_at")
    EXTW = 416
    es = ExitStack()
    with es:
        def sb(name, shape, dt):
            return es.enter_context(nc.sbuf_tensor("sb_" + name, shape, dt))

        def psum(name, shape, dt):
            return es.enter_context(nc.psum_tensor("ps_" + name, shape, dt))

        xT = sb("xT", [128, KD, TMAX], F32)
        hT = sb("hT", [128, KD, TMAX], BF16)
        big_f = sb("big", [128, KD, TMAX], F32)
        big_b = sb_b = big_f.bitcast(BF16)
        wsl = sb("wsl", [128, NW, KD * 128], BF16)
        pp = sb("pp", [128, cfg.ppw], F32)
        Hg_p = sb("Hg_p", [128, NB * KD, HB], F32)
        Hg_s = sb("Hg_s", [128, KD, HB], F32)
        Hu_p = sb("Hu_p", [128, depth * KF, HF], F32)
        Hu_s = sb("Hu_s", [128, KF, HF], F32)
        ext = sb("ext", [128, 4, EXTW], F32)
        scr = sb("scr", [128, 9, TMAX], F32)
        scr_b = scr.bitcast(BF16)
        ident_f = sb("ident_f", [128, 128], F32)
        ident_b = sb("ident_b", [128, 128], BF16)
        ones_b = sb("ones_b", [128, 128], BF16)
        wst_b = sb("wst_b", [128, GA * 128], BF16)
        bs_b = sb("bs_b", [1, 2 * GA * 128], BF16)
        so_f = sb("so_f", [128, 2, 512], F32)
        so_flat = so_f[:].rearrange("p a b -> p (a b)")
        ext_flat = ext[:].rearrange("p a b -> p (a b)")
        valid = sb("valid_sb", [128, 1], F32)
        epsc = sb("epsc", [128, 1], F32)

        pm = psum("pm", [128, 3, 512], F32)
        pst = psum("pst", [128, 2, 512], F32)
        ptr = psum("ptr", [128, 2, 512], F32)
        ptb = psum("ptb", [128, 1, 1024], BF16)

        def bigb(j, c0, c1):
            return big_b[:, j // 2, (j % 2) * TMAX + c0:(j % 2) * TMAX + c1]

        hT_flat = hT[:].rearrange("p k t -> p (k t)")
        big_flat = big_f[:].rearrange("p k t -> p (k t)")

        def vtok(s, r0, r1, c0, c1):
            return hT_flat[r0:r1, s * D + c0:s * D + c1]

        def vtok_keys(s):
            lo = (s * D) // TMAX
            hi = ((s + 1) * D + TMAX - 1) // TMAX
            return [("h", k) for k in range(lo, min(hi, KD))]

        def stage(s, r0, r1, c0, c1):
            return big_flat[r0:r1, s * D + c0:s * D + c1]

        def stage_keys(s):
            lo = (s * D * 4) // (TMAX * 2)
            hi = ((s + 1) * D * 4 + TMAX * 2 - 1) // (TMAX * 2)
            return [("big", j) for j in range(lo, min(hi, 2 * KD))]

        NSTAGE = 2
        rot = {}

        def nxt(name, n):
            v = rot.get(name, 0)
            rot[name] = v + 1
            return v % n

        P.dma("sp", lambda e: e.dma_start(out=ident_f[:], in_=ident_d), "c_ident", writes=["ident_f"])
        P.dma("sp", lambda e: e.dma_start(out=valid[:], in_=valid_d), "c_valid", writes=["valid"])
        P.op("act", lambda e: e.copy(out=ident_b[:], in_=ident_f[:]), reads=["ident_f"], writes=["ident_b"])
        P.op("dve", lambda e: e.memset(ones_b[:], 1.0), writes=["ones_b"])
        P.op("dve", lambda e: e.memset(epsc[:], EPS), writes=["epsc"])
        P.op("dve", lambda e: e.memset(Hg_p[:], 0.0), writes=[("Hg_p", j, c) for j in range(NB) for c in range(KD)])
        P.op("dve", lambda e: e.memset(Hu_p[:], 0.0), writes=[("Hu_p", i, c) for i in range(depth) for c in range(KF)])

        wstate = dict(next_dma=0, next_use=0)

        def w_dma(u):
            slot = u % NW
            pas, uu = divmod(u, cfg.NU)
            early = (uu % 3 == 0)
            rows = slice(uu * 128, (uu + 1) * 128)
            wc = wcs[uu // upc]
            crow = slice((uu % upc) * 128, (uu % upc + 1) * 128)
            from_cache = (pas >= 2) or (pas == 1 and early)
            if from_cache:
                P.dma("pool", lambda e: e.dma_start(out=wsl[:, slot, :], in_=wc[crow, :]),
                      ("w", slot), reads=[("wc", uu)], writes=[("w", slot)])
                return
            P.dma("pool", lambda e: e.dma_start(out=wsl[:, slot, :], in_=ws[rows, :]),
                  ("w", slot), writes=[("w", slot)])
            if len(cfg.tiles) > 2 and ((pas == 0 and early) or pas == 1):
                P.dma("sp", lambda e: e.dma_start(out=wc[crow, :], in_=wsl[:, slot, :]),
                      ("wb", slot), reads=[("w", slot)], writes=[("wc", uu)])

        def mm_unit(bank, T, rhs_fn, rhs_keys, start=True, stop=True):
            u = wstate["next_use"]
            wstate["next_use"] = u + 1
            while wstate["next_dma"] <= min(u + NW - 1, total_units - 1):
                w_dma(wstate["next_dma"])
                wstate["next_dma"] += 1
            slot = u % NW

            def fn(e):
                ins = None
                for k in range(KD):
                    ins = e.matmul(pm[:, bank, 0:T], wsl[:, slot, k * 128:(k + 1) * 128], rhs_fn(k),
                                   start=(start and k == 0), stop=(stop and k == KD - 1))
                return ins
            P.op("pe", fn, reads=[("w", slot)] + rhs_keys, writes=[("pm", bank)])

        total_units = cfg.NU * len(cfg.tiles)

        def pcol(off, n=1):
            return pp[:, off:off + n]

        def load_pp(layer):
            P.dma("sp", lambda e: e.dma_start(out=pp[:], in_=pp_d[layer * 128:(layer + 1) * 128, :]),
                  "pp", writes=["pp"])

        def stat_finish_rms(T, dst):
            P.op("act", lambda e: e.activation(out=scr[:, dst, 0:T], in_=pst[:, 0, 0:T], func=AF.Sqrt, bias=epsc[:, 0:1], scale=1.0 / D),
                 reads=[("pst", 0), "epsc"], writes=[("scr", dst)])
            P.op("dve", lambda e: e.reciprocal(out=scr[:, dst, 0:T], in_=scr[:, dst, 0:T]),
                 reads=[("scr", dst)], writes=[("scr", dst)])

        def rms_in(T, goff):
            for k in range(KD):
                s = 6 + nxt("sq", 2)
                P.op("act", lambda e, k=k, s=s: e.activation(out=scr_b[:, s, 0:T], in_=xT[:, k, 0:T], func=AF.Square),
                     reads=[("x", k)], writes=[("scr", s)])
                P.op("pe", lambda e, k=k, s=s: e.matmul(pst[:, 0, 0:T], ones_b[:, :], scr_b[:, s, 0:T],
                                                        start=(k == 0), stop=(k == KD - 1)),
                     reads=[("scr", s), "ones_b"], writes=[("pst", 0)])
            stat_finish_rms(T, 8)
            for k in range(KD):
                P.op("dve", lambda e, k=k: e.scalar_tensor_tensor(
                    out=hT[:, k, 0:T], in0=xT[:, k, 0:T], scalar=pcol(goff + k), in1=scr[:, 8, 0:T],
                    op0=ALU.mult, op1=ALU.mult),
                    reads=[("x", k), ("scr", 8), "pp"], writes=[("h", k)])

        pend = []

        def flush_pend(keep=0):
            while len(pend) > keep:
                pend.pop(0)()

        def post_evac(bank, T, j):
            s = 6 + nxt("sq", 2)
            P.op("act", lambda e: e.activation(out=scr_b[:, s, 0:T], in_=pm[:, bank, 0:T], func=AF.Square),
                 reads=[("pm", bank)], writes=[("scr", s)])
            P.op("dve", lambda e: e.tensor_copy(out=hT[:, j, 0:T], in_=pm[:, bank, 0:T]),
                 reads=[("pm", bank)], writes=[("h", j)])
            pend.append(lambda: P.op("pe", lambda e: e.matmul(pst[:, 0, 0:T], ones_b[:, :], scr_b[:, s, 0:T],
                                                              start=(j == 0), stop=(j == KD - 1)),
                                     reads=[("scr", s), "ones_b"], writes=[("pst", 0)]))

        def post_finish(T, goff):
            flush_pend()
            stat_finish_rms(T, 8)
            for j in range(KD):
                s = nxt("tmpx", 2)
                P.op("dve", lambda e, j=j, s=s: e.scalar_tensor_tensor(
                    out=scr[:, s, 0:T], in0=hT[:, j, 0:T], scalar=pcol(goff + j), in1=scr[:, 8, 0:T],
                    op0=ALU.mult, op1=ALU.mult),
                    reads=[("h", j), ("scr", 8), "pp"], writes=[("scr", s)])
                P.op("dve", lambda e, j=j, s=s: e.tensor_tensor(
                    out=xT[:, j, 0:T], in0=xT[:, j, 0:T], in1=scr[:, s, 0:T], op=ALU.add),
                    reads=[("x", j), ("scr", s)], writes=[("x", j)])

        def ln_stats_finish(T, sc):
            P.op("dve", lambda e: e.tensor_scalar(out=scr[:, 8, 0:T], in0=pst[:, 0, 0:T], scalar1=sc,
                                                  scalar2=None, op0=ALU.mult),
                 reads=[("pst", 0)], writes=[("scr", 8)])
            P.op("dve", lambda e: e.tensor_tensor(out=scr[:, 6, 0:T], in0=scr[:, 8, 0:T], in1=scr[:, 8, 0:T],
                                                  op=ALU.mult),
                 reads=[("scr", 8)], writes=[("scr", 6)])
            P.op("dve", lambda e: e.scalar_tensor_tensor(out=scr[:, 7, 0:T], in0=pst[:, 1, 0:T], scalar=sc,
                                                         in1=scr[:, 6, 0:T], op0=ALU.mult, op1=ALU.subtract),
                 reads=[("pst", 1), ("scr", 6)], writes=[("scr", 7)])
            P.op("act", lambda e: e.activation(out=scr[:, 7, 0:T], in_=scr[:, 7, 0:T], func=AF.Sqrt, bias=epsc[:, 0:1]),
                 reads=[("scr", 7), "epsc"], writes=[("scr", 7)])
            P.op("dve", lambda e: e.reciprocal(out=scr[:, 7, 0:T], in_=scr[:, 7, 0:T]),
                 reads=[("scr", 7)], writes=[("scr", 7)])
            P.op("dve", lambda e: e.scalar_tensor_tensor(out=scr[:, 8, 0:T], in0=scr[:, 8, 0:T], scalar=-1.0,
                                                         in1=scr[:, 7, 0:T], op0=ALU.mult, op1=ALU.mult),
                 reads=[("scr", 8), ("scr", 7)], writes=[("scr", 8)])

        def out_rows_T(src_fn, src_keys, nchunks, nrows, dst, dst_row0):
            for r0 in range(0, nrows, 4):
                nr = min(4, nrows - r0)
                b = nxt("ptr", 2)
                s = nxt("so", 2)

                def ft(e, nr=nr, r0=r0, b=b):
                    ins = None
                    for rr in range(nr):
                        ins = e.transpose(ptr[0:nchunks, b, rr * 128:(rr + 1) * 128], src_fn(r0 + rr),
                                          ident_f[:, :])
                    return ins
                P.op("pe", ft, reads=src_keys + ["ident_f"], writes=[("ptr", b)])
                P.op("act", lambda e, nr=nr, b=b, s=s: e.copy(out=so_f[0:nchunks, s, 0:nr * 128], in_=ptr[0:nchunks, b, 0:nr * 128]),
                     reads=[("ptr", b)], writes=[("so", s)])
                P.dma("sp", lambda e, nr=nr, r0=r0, s=s: e.dma_start(
                    out=dst[dst_row0 + r0:dst_row0 + r0 + nr, :].rearrange("r (c p) -> c r p", p=128),
                    in_=so_f[0:nchunks, s, 0:nr * 128].rearrange("c (r p) -> c r p", p=128)),
                    ("so", s), reads=[("so", s)])

        def in_rows_T(src, src_row0, nrows, nchunks, dst_fn, dst_keys):
            for c0 in range(0, nchunks, 4):
                b = nxt("ptr", 2)
                s = nxt("so", 2)
                P.dma("sp", lambda e, s=s, c0=c0: e.dma_start(out=so_f[0:nrows, s, :],
                                                   in_=src[src_row0:src_row0 + nrows, c0 * 128:(c0 + 4) * 128]),
                      ("so", s), writes=[("so", s)])

                def ft(e, s=s, b=b):
                    ins = None
                    for q in range(4):
                        ins = e.transpose(ptr[:, b, q * 128:q * 128 + nrows], so_f[0:nrows, s, q * 128:(q + 1) * 128],
                                          ident_f[0:nrows, 0:nrows])
                    return ins
                P.op("pe", ft, reads=[("so", s), "ident_f"], writes=[("ptr", b)])
                P.op("act", lambda e, c0=c0, b=b: e.copy(
                    out=dst_fn(c0, 4),
                    in_=ptr[:, b, :].rearrange("p (q r) -> p q r", r=128)[:, :, 0:nrows]),
                    reads=[("ptr", b)], writes=[(dst_keys, c0 + q) for q in range(4)])

        def tile_load(tile):
            for (kind, col0, ln, src0, own0) in tile["segs"]:
                src = xp if kind == "p" else xs
                for r0 in range(0, ln, 128):
                    rows = min(128, ln - r0)
                    s = nxt("stage", NSTAGE)
                    P.dma("sp", lambda e, s=s, rows=rows, r0=r0, src=src, src0=src0: e.dma_start(
                        out=stage(s, 0, rows, 0, D), in_=src[src0 + r0:src0 + r0 + rows, :]),
                        ("stage", s), writes=stage_keys(s))
                    for k0 in range(0, KD, 4):
                        b = nxt("ptr", 2)

                        def ft(e, s=s, rows=rows, k0=k0, b=b):
                            ins = None
                            for q in range(4):
                                ins = e.transpose(ptr[:, b, q * 128:q * 128 + rows],
                                                  stage(s, 0, rows, (k0 + q) * 128, (k0 + q + 1) * 128),
                                                  ident_f[0:rows, 0:rows])
                            return ins
                        P.op("pe", ft, reads=stage_keys(s) + ["ident_f"], writes=[("ptr", b)])
                        P.op("act", lambda e, rows=rows, k0=k0, b=b, c=col0 + r0: e.copy(
                            out=xT[:, k0:k0 + 4, c:c + rows],
                            in_=ptr[:, b, :].rearrange("p (q r) -> p q r", r=128)[:, :, 0:rows]),
                            reads=[("ptr", b)], writes=[("x", k0 + q) for q in range(4)])

        def tile_store(tile):
            for (kind, col0, ln, src0, own0) in tile["segs"]:
                if own0 is None:
                    continue
                dst = yp if kind == "p" else ys
                for r0 in range(0, ln, 128):
                    rows = min(128, ln - r0)
                    s = nxt("stage", NSTAGE)
                    for k0 in range(0, KD, 4):
                        b = nxt("ptr", 2)

                        def ft(e, rows=rows, k0=k0, b=b, c=col0 + r0):
                            ins = None
                            for q in range(4):
                                ins = e.transpose(ptr[0:rows, b, q * 128:(q + 1) * 128],
                                                  xT[:, k0 + q, c:c + rows], ident_f[:, :])
                            return ins
                        P.op("pe", ft, reads=[("x", k0 + q) for q in range(4)] + ["ident_f"], writes=[("ptr", b)])
                        P.op("act", lambda e, s=s, rows=rows, k0=k0, b=b: e.copy(
                            out=stage(s, 0, rows, k0 * 128, (k0 + 4) * 128), in_=ptr[0:rows, b, :]),
                            reads=[("ptr", b)], writes=stage_keys(s))
                    P.dma("sp", lambda e, s=s, rows=rows, dst=dst, o=own0 + r0: e.dma_start(
                        out=dst[o:o + rows, :], in_=stage(s, 0, rows, 0, D)),
                        ("stage", s), reads=stage_keys(s))

        def ext_layout(tile, H):
            segs = []
            o = 0
            for (kind, col0, ln, src0, own0) in tile["segs"]:
                segs.append((kind, col0, ln, o))
                o += H + ln
            return segs, o

        def ffn(ti, tile, layer):
            T = tile["T"]
            last_p = (ti == len(cfg.tiles) - 1)
            P.mark("ffn.start")
            rms_in(T, cfg.o_fpre)
            P.mark("ffn.rms")
            segs, W = ext_layout(tile, HF)
            if ti == 0:
                in_rows_T(sffn, layer * HF, HF, KF, lambda c0, n: Hu_s[:, c0:c0 + n, :], "Hu_s")
            P.mark("ffn.inrows")
            for c in range(KH):
                par = c % 2
                if c == 1:
                    P.mark("ffn.pair0")
                for half in range(2):
                    cc = c + half * KH
                    bank = nxt("pm", 3)
                    mm_unit(bank, T, lambda k: hT[:, k, 0:T], [("h", k) for k in range(KD)])
                    eb = 2 * par + half
                    cv = 2 * half + par
                    for (kind, col0, ln, eo) in segs:
                        if kind == "p":
                            Hsrc = Hu_p[:, layer * KF + cc, :]
                            hk = ("Hu_p", layer, cc)
                        else:
                            Hsrc = Hu_s[:, cc, :]
                            hk = ("Hu_s", cc)
                        P.op("dve", lambda e, eb=eb, eo=eo, Hsrc=Hsrc: e.tensor_copy(out=ext[:, eb, eo:eo + HF], in_=Hsrc),
                             reads=[hk], writes=[("ext", eb)])
                        P.op("act", lambda e, eb=eb, eo=eo, ln=ln, col0=col0, bank=bank: e.copy(
                            out=ext[:, eb, eo + HF:eo + HF + ln], in_=pm[:, bank, col0:col0 + ln]),
                            reads=[("pm", bank)], writes=[("ext", eb)])
                        if kind == "p" and ti == 0:
                            P.op("dve", lambda e, eb=eb, eo=eo, ln=ln, Hsrc=Hsrc: e.tensor_scalar(
                                out=Hsrc, in0=ext[:, eb, eo + ln:eo + ln + HF], scalar1=valid[:, 0:1], scalar2=None,
                                op0=ALU.mult), reads=[("ext", eb), "valid"], writes=[hk])
                        else:
                            P.op("dve", lambda e, eb=eb, eo=eo, ln=ln, Hsrc=Hsrc: e.tensor_copy(
                                out=Hsrc, in_=ext[:, eb, eo + ln:eo + ln + HF]), reads=[("ext", eb)], writes=[hk])
                    wo = cfg.o_fw + cc * CF
                    P.op("act", lambda e, eb=eb, cv=cv, wo=wo, cc=cc: e.activation(
                        out=scr[:, cv, 0:W - HF], in_=ext[:, eb, HF:W], func=AF.Identity,
                        scale=pcol(wo + 2), bias=pcol(cfg.o_fb + cc)),
                        reads=[("ext", eb), "pp"], writes=[("scr", cv)])
                    for k in (1, 0):
                        P.op("dve", lambda e, eb=eb, cv=cv, wo=wo, k=k: e.scalar_tensor_tensor(
                            out=scr[:, cv, 0:W - HF], in0=ext[:, eb, k:k + W - HF], scalar=pcol(wo + k),
                            in1=scr[:, cv, 0:W - HF], op0=ALU.mult, op1=ALU.add),
                            reads=[("ext", eb), ("scr", cv), "pp"], writes=[("scr", cv)])
                cg, cvv = par, 2 + par
                P.op("act", lambda e, cg=cg: e.activation(out=scr[:, cg, 0:W - HF], in_=scr[:, cg, 0:W - HF], func=AF.Silu),
                     reads=[("scr", cg)], writes=[("scr", cg)])
                for (kind, col0, ln, eo) in segs:
                    P.op("dve", lambda e, cg=cg, cvv=cvv, c=c, col0=col0, ln=ln, eo=eo: e.tensor_tensor(
                        out=bigb(c, col0, col0 + ln), in0=scr[:, cg, eo:eo + ln], in1=scr[:, cvv, eo:eo + ln],
                        op=ALU.mult), reads=[("scr", cg), ("scr", cvv)], writes=[("big", c)])
            P.mark("ffn.up")
            if ti == 0:
                out_rows_T(lambda r: Hu_s[:, :, r], [("Hu_s", c) for c in range(KF)], KF, HF, nfs, layer * HF)
            if last_p:
                out_rows_T(lambda r: Hu_p[:, layer * KF:(layer + 1) * KF, r], [("Hu_p", layer, c) for c in range(KF)], KF, HF, nfp, layer * HF)
            P.mark("ffn.outrows")
            for j in range(KD):
                bank = nxt("pm", 3)
                for half in range(2):
                    mm_unit(bank, T, lambda k, half=half: bigb(half * KD + k, 0, T),
                            [("big", half * KD + k) for k in range(KD)], start=(half == 0), stop=(half == 1))
                flush_pend()
                post_evac(bank, T, j)
            P.mark("ffn.down")
            post_finish(T, cfg.o_fpost)
            P.mark("ffn.end")

        def gmlp_blocks(tile):
            blocks = []
            for (kind, col0, ln, src0, own0) in tile["segs"]:
                for r0 in range(0, ln, 128):
                    blocks.append((col0 + r0, min(128, ln - r0), kind))
            return blocks

        def gmlp(ti, tile, layer):
            T = tile["T"]
            ja = layer // 2
            om = cfg.o_mix
            o_bin, o_lng, o_lnb = om, om + 2 * KD, om + 3 * KD
            SO = [("so", 0), ("so", 1)]
            EX = [("ext", 0), ("ext", 1), ("ext", 2)]
            P.dma("sp", lambda e: e.dma_start(out=so_flat[:, :], in_=wst_d[ja * 128:(ja + 1) * 128, :]), "wst",
                  writes=SO)
            P.op("act", lambda e: e.copy(out=wst_b[:], in_=so_flat[:, :]), reads=SO, writes=["wst_b"])
            wv = wst_b[:].rearrange("p (g i) -> p g i", i=128)
            P.op("dve", lambda e: e.memset(wv[64:128, :, 0:64], 0.0), reads=["wst_b"], writes=["wst_b"])
            P.dma("sp", lambda e: e.dma_start(out=ext_flat[0:1, 0:GA * 128], in_=bs_d[ja:ja + 1, :]), "bsd", writes=EX)
            P.op("act", lambda e: e.copy(out=bs_b[:, 0:GA * 128], in_=ext_flat[0:1, 0:GA * 128]), reads=EX, writes=["bs_b"])
            P.op("act", lambda e: e.copy(out=so_flat[0:1, :], in_=bs_b[:, 0:GA * 128]), reads=["bs_b"], writes=SO)
            P.op("dve", lambda e: e.tensor_tensor(out=bs_b[:, GA * 128:2 * GA * 128], in0=ext_flat[0:1, 0:GA * 128],
                                                  in1=so_flat[0:1, :], op=ALU.subtract),
                 reads=EX + SO + ["bs_b"], writes=["bs_b"])
            rms_in(T, cfg.o_mpre)
            hk = [("h", k) for k in range(KD)]
            for c in range(KD):
                bank = nxt("pm", 3)
                mm_unit(bank, T, lambda k: hT[:, k, 0:T], hk)
                flush_pend()
                P.op("act", lambda e, c=c, bank=bank: e.activation(
                    out=bigb(KD + c, 0, T), in_=pm[:, bank, 0:T], func=AF.Gelu, bias=pcol(o_bin + KD + c)),
                    reads=[("pm", bank), "pp"], writes=[("big", KD + c)])
                s = 4 + nxt("sqb", 2)
                P.op("act", lambda e, c=c, s=s: e.activation(out=scr_b[:, s, 0:T], in_=bigb(KD + c, 0, T), func=AF.Square),
                     reads=[("big", KD + c)], writes=[("scr", s)])

                def vstats(c=c, s=s):
                    P.op("pe", lambda e: e.matmul(pst[:, 0, 0:T], ones_b[:, :], bigb(KD + c, 0, T),
                                                  start=(c == 0), stop=(c == KD - 1)),
                         reads=[("big", KD + c), "ones_b"], writes=[("pst", 0)])
                    P.op("pe", lambda e: e.matmul(pst[:, 1, 0:T], ones_b[:, :], scr_b[:, s, 0:T],
                                                  start=(c == 0), stop=(c == KD - 1)),
                         reads=[("scr", s), "ones_b"], writes=[("pst", 1)])
                pend.append(vstats)
            flush_pend()
            ln_stats_finish(T, 1.0 / D)
            for c in range(KD):
                bank = nxt("pm", 3)
                mm_unit(bank, T, lambda k: hT[:, k, 0:T], hk)
                P.op("act", lambda e, c=c, bank=bank: e.activation(
                    out=bigb(c, 0, T), in_=pm[:, bank, 0:T], func=AF.Gelu, bias=pcol(o_bin + c)),
                    reads=[("pm", bank), "pp"], writes=[("big", c)])
                s = nxt("nrm", 2)
                P.op("dve", lambda e, c=c, s=s: e.tensor_tensor(out=scr[:, s, 0:T], in0=bigb(KD + c, 0, T),
                                                                in1=scr[:, 7, 0:T], op=ALU.mult),
                     reads=[("big", KD + c), ("scr", 7)], writes=[("scr", s)])
                P.op("dve", lambda e, s=s: e.tensor_tensor(out=scr[:, s, 0:T], in0=scr[:, s, 0:T], in1=scr[:, 8, 0:T],
                                                           op=ALU.add),
                     reads=[("scr", s), ("scr", 8)], writes=[("scr", s)])
                P.op("act", lambda e, c=c, s=s: e.activation(
                    out=bigb(KD + c, 0, T), in_=scr[:, s, 0:T], func=AF.Identity,
                    scale=pcol(o_lng + c), bias=pcol(o_lnb + c)),
                    reads=[("scr", s), "pp"], writes=[("big", KD + c)])
            for (bc, rows, kind) in gmlp_blocks(tile):
                vs = nxt("vtok", 3)
                vk = vtok_keys(vs)
                for c0 in range(0, KD, 8):
                    def ft(e, c0=c0, bc=bc, rows=rows):
                        ins = None
                        for q in range(8):
                            ins = e.transpose(ptb[0:rows, 0, q * 128:(q + 1) * 128],
                                              bigb(KD + c0 + q, bc, bc + rows), ident_b[:, :])
                        return ins
                    P.op("pe", ft, reads=[("big", KD + c0 + q) for q in range(8)] + ["ident_b"], writes=["ptb"])
                    P.op("act", lambda e, c0=c0, rows=rows, vs=vs: e.copy(
                        out=vtok(vs, 0, rows, c0 * 128, (c0 + 8) * 128), in_=ptb[0:rows, 0, :]),
                        reads=["ptb"], writes=vk)
                if kind == "s":
                    for g0 in range(0, D, 512):
                        s = nxt("so", 2)
                        P.op("act", lambda e, s=s, g0=g0, vs=vs, rows=rows: e.copy(
                            out=so_f[0:rows, s, :], in_=vtok(vs, 0, rows, g0, g0 + 512)),
                            reads=vk, writes=[("so", s)])
                        P.dma("sp", lambda e, s=s, g0=g0, rows=rows: e.dma_start(
                            out=nvs[ja * NS:ja * NS + rows, g0:g0 + 512], in_=so_f[0:rows, s, :]),
                            ("so", s), reads=[("so", s)])
                for c0 in range(0, KD, 4):
                    b = nxt("ptr", 2)

                    def fm(e, c0=c0, b=b, rows=rows, vs=vs):
                        ins = None
                        for q in range(4):
                            c = c0 + q
                            g = c // cpg
                            o = ptr[:, b, q * 128:q * 128 + rows]
                            e.matmul(o, vtok(vs, 0, rows, c * 128, (c + 1) * 128),
                                     wst_b[0:rows, g * 128:g * 128 + rows], start=True, stop=False)
                            e.matmul(o, ones_b[0:1, :], bs_b[0:1, g * 128:g * 128 + rows], start=False, stop=False)
                            ins = e.matmul(o, ones_b[0:1, :], bs_b[0:1, (GA + g) * 128:(GA + g) * 128 + rows],
                                           start=False, stop=True)
                        return ins
                    P.op("pe", fm, reads=vk + ["wst_b", "bs_b", "ones_b"], writes=[("ptr", b)])
                    P.op("dve", lambda e, c0=c0, b=b, rows=rows, bc=bc: e.tensor_tensor(
                        out=big_b[:, c0 // 2:c0 // 2 + 2, :].rearrange("p a (h t) -> p (a h) t", h=2)[:, :, bc:bc + rows],
                        in0=big_b[:, c0 // 2:c0 // 2 + 2, :].rearrange("p a (h t) -> p (a h) t", h=2)[:, :, bc:bc + rows],
                        in1=ptr[:, b, :].rearrange("p (q r) -> p q r", r=128)[:, :, 0:rows], op=ALU.mult),
                        reads=[("ptr", b)] + [("big", c0 + q) for q in range(4)],
                        writes=[("big", c0 + q) for q in range(4)])
            for j in range(KD):
                bank = nxt("pm", 3)
                mm_unit(bank, T, lambda k: bigb(k, 0, T), [("big", k) for k in range(KD)])
                flush_pend()
                post_evac(bank, T, j)
            post_finish(T, cfg.o_mpost)

        def conformer(ti, tile, layer):
            T = tile["T"]
            jb = layer // 2
            last_p = (ti == len(cfg.tiles) - 1)
            om = cfg.o_mix
            o_bin, o_w, o_b = om, om + 2 * KD, om + 2 * KD + CB * KD
            o_lng, o_lnb = o_b + KD, o_b + 2 * KD
            rms_in(T, cfg.o_mpre)
            segs, W = ext_layout(tile, HB)
            if ti == 0:
                in_rows_T(sconv, jb * HB, HB, KD, lambda c0, n: Hg_s[:, c0:c0 + n, :], "Hg_s")
            hk = [("h", k) for k in range(KD)]
            for c in range(KD):
                bA = nxt("pm", 3)
                mm_unit(bA, T, lambda k: hT[:, k, 0:T], hk)
                bB = nxt("pm", 3)
                mm_unit(bB, T, lambda k: hT[:, k, 0:T], hk)
                flush_pend(keep=2)
                sg = 4 + nxt("sig", 2)
                P.op("act", lambda e, c=c, bB=bB, sg=sg: e.activation(
                    out=scr[:, sg, 0:T], in_=pm[:, bB, 0:T], func=AF.Sigmoid, bias=pcol(o_bin + KD + c)),
                    reads=[("pm", bB), "pp"], writes=[("scr", sg)])
                eb = nxt("gext", 2)
                for (kind, col0, ln, eo) in segs:
                    if kind == "p":
                        Hsrc = Hg_p[:, jb * KD + c, :]
                        hkey = ("Hg_p", jb, c)
                    else:
                        Hsrc = Hg_s[:, c, :]
                        hkey = ("Hg_s", c)
                    P.op("act", lambda e, eb=eb, eo=eo, Hsrc=Hsrc: e.copy(out=ext[:, eb, eo:eo + HB], in_=Hsrc),
                         reads=[hkey], writes=[("ext", eb)])
                    P.op("dve", lambda e, eb=eb, eo=eo, ln=ln, col0=col0, bA=bA, sg=sg, c=c: e.scalar_tensor_tensor(
                        out=ext[:, eb, eo + HB:eo + HB + ln], in0=pm[:, bA, col0:col0 + ln], scalar=pcol(o_bin + c),
                        in1=scr[:, sg, col0:col0 + ln], op0=ALU.add, op1=ALU.mult),
                        reads=[("pm", bA), ("scr", sg), "pp"], writes=[("ext", eb)])
                    if kind == "p" and ti == 0:
                        P.op("dve", lambda e, eb=eb, eo=eo, ln=ln, Hsrc=Hsrc: e.tensor_scalar(
                            out=Hsrc, in0=ext[:, eb, eo + ln:eo + ln + HB], scalar1=valid[:, 0:1], scalar2=None,
                            op0=ALU.mult), reads=[("ext", eb), "valid"], writes=[hkey])
                    else:
                        P.op("dve", lambda e, eb=eb, eo=eo, ln=ln, Hsrc=Hsrc: e.tensor_copy(
                            out=Hsrc, in_=ext[:, eb, eo + ln:eo + ln + HB]), reads=[("ext", eb)], writes=[hkey])
                ac = 2 + nxt("acc", 2)
                Wc = W - HB
                P.op("act", lambda e, eb=eb, ac=ac, c=c: e.activation(
                    out=scr[:, ac, 0:Wc], in_=ext[:, eb, HB:W], func=AF.Identity,
                    scale=pcol(o_w + c * CB + HB), bias=pcol(o_b + c)),
                    reads=[("ext", eb), "pp"], writes=[("scr", ac)])
                for k in range(HB - 1):
                    P.op("dve", lambda e, eb=eb, ac=ac, c=c, k=k: e.scalar_tensor_tensor(
                        out=scr[:, ac, 0:Wc], in0=ext[:, eb, k:k + Wc], scalar=pcol(o_w + c * CB + k),
                        in1=scr[:, ac, 0:Wc], op0=ALU.mult, op1=ALU.add),
                        reads=[("ext", eb), ("scr", ac), "pp"], writes=[("scr", ac)])
                k = HB - 1
                for (kind, col0, ln, eo) in segs:
                    P.op("dve", lambda e, eb=eb, ac=ac, c=c, k=k, eo=eo, ln=ln, col0=col0: e.scalar_tensor_tensor(
                        out=big_f[:, c, col0:col0 + ln], in0=ext[:, eb, k + eo:k + eo + ln], scalar=pcol(o_w + c * CB + k),
                        in1=scr[:, ac, eo:eo + ln], op0=ALU.mult, op1=ALU.add),
                        reads=[("ext", eb), ("scr", ac), "pp"], writes=[("big", 2 * c), ("big", 2 * c + 1)])
                def cstats(c=c):
                    s = 6 + nxt("sq", 2)
                    P.op("act", lambda e: e.copy(out=scr_b[:, s, 0:T], in_=big_f[:, c, 0:T]),
                         reads=[("big", 2 * c), ("big", 2 * c + 1)], writes=[("scr", s)])
                    P.op("pe", lambda e: e.matmul(pst[:, 0, 0:T], ones_b[:, :], scr_b[:, s, 0:T],
                                                  start=(c == 0), stop=(c == KD - 1)),
                         reads=[("scr", s), "ones_b"], writes=[("pst", 0)])
                    s2 = 6 + nxt("sq", 2)
                    P.op("act", lambda e: e.activation(out=scr_b[:, s2, 0:T], in_=big_f[:, c, 0:T], func=AF.Square),
                         reads=[("big", 2 * c), ("big", 2 * c + 1)], writes=[("scr", s2)])
                    P.op("pe", lambda e: e.matmul(pst[:, 1, 0:T], ones_b[:, :], scr_b[:, s2, 0:T],
                                                  start=(c == 0), stop=(c == KD - 1)),
                         reads=[("scr", s2), "ones_b"], writes=[("pst", 1)])
                pend.append(cstats)
            flush_pend()
            if ti == 0:
                out_rows_T(lambda r: Hg_s[:, :, r], [("Hg_s", c) for c in range(KD)], KD, HB, ncs, jb * HB)
            if last_p:
                out_rows_T(lambda r: Hg_p[:, jb * KD:(jb + 1) * KD, r], [("Hg_p", jb, c) for c in range(KD)], KD, HB, ncp, jb * HB)
            ln_stats_finish(T, 1.0 / D)
            for c in range(KD):
                s = nxt("nrm", 2)
                P.op("dve", lambda e, c=c, s=s: e.tensor_tensor(out=scr[:, s, 0:T], in0=big_f[:, c, 0:T],
                                                                in1=scr[:, 7, 0:T], op=ALU.mult),
                     reads=[("big", 2 * c), ("big", 2 * c + 1), ("scr", 7)], writes=[("scr", s)])
                P.op("dve", lambda e, s=s: e.tensor_tensor(out=scr[:, s, 0:T], in0=scr[:, s, 0:T], in1=scr[:, 8, 0:T],
                                                           op=ALU.add),
                     reads=[("scr", s), ("scr", 8)], writes=[("scr", s)])
                P.op("act", lambda e, c=c, s=s: e.activation(
                    out=bigb(c, 0, T), in_=scr[:, s, 0:T], func=AF.Silu, scale=pcol(o_lng + c), bias=pcol(o_lnb + c)),
                    reads=[("scr", s), "pp"], writes=[("big", c)])
            for j in range(KD):
                bank = nxt("pm", 3)
                mm_unit(bank, T, lambda k: bigb(k, 0, T), [("big", k) for k in range(KD)])
                flush_pend()
                post_evac(bank, T, j)
            post_finish(T, cfg.o_mpost)

        dbg = getattr(cfg, "dbg", None) or {}
        for ti, tile in enumerate(cfg.tiles[:dbg.get("ntiles", 99)]):
            tile_load(tile)
            for layer in range(min(depth, dbg.get("nlayers", 99))):
                load_pp(layer)
                if dbg.get("mixer", True):
                    if layer % 2 == 0:
                        gmlp(ti, tile, layer)
                    else:
                        conformer(ti, tile, layer)
                if dbg.get("ffn", True):
                    ffn(ti, tile, layer)
            tile_store(tile)
        if not dbg:
            assert wstate["next_use"] == total_units, (wstate, total_units)

        n_ep = {e: (P.cnt[e] + EPOCH - 1) // EPOCH for e in ENG}
        esem = {e: [es.enter_context(nc.semaphore(f"s_{e}{i}")) for i in range(n_ep[e])] for e in ENG}
        dsem = {k: es.enter_context(nc.semaphore("d_" + "_".join(str(x) for x in (k if isinstance(k, tuple) else (k,)))))
                for k in P.dcnt}
        fin = es.enter_context(nc.semaphore("fin"))

        def emit(eng_name, e):
            seq = 0
            for (waits, fn, dkey) in P.ops[eng_name]:
                for (key, val) in waits:
                    if isinstance(key, tuple) and key[0] == "d":
                        e.wait_ge(dsem[key[1]], 16 * val)
                    else:
                        e.wait_ge(esem[key][val // EPOCH], val % EPOCH + 1)
                ins = fn(e)
                if dkey is None:
                    ins.then_inc(esem[eng_name][seq // EPOCH], 1)
                    seq += 1
                else:
                    ins.then_inc(dsem[dkey], 16)
            if eng_name == "sp":
                for k, n in P.dcnt.items():
                    e.wait_ge(dsem[k], 16 * n)
                for other in ("pe", "act", "dve"):
                    c = P.cnt[other]
                    if c:
                        e.wait_ge(esem[other][(c - 1) // EPOCH], (c - 1) % EPOCH + 1)

        with nc.Block() as block:
            @block.tensor
            def _(e):
                emit("pe", e)

            @block.scalar
            def _(e):
                emit("act", e)

            @block.vector
            def _(e):
                emit("dve", e)

            @block.gpsimd
            def _(e):
                emit("pool", e)

            @block.sync
            def _(e):
                emit("sp", e)
    return nc


def _fm(v, K):
    return np.ascontiguousarray(np.asarray(v, np.float32).reshape(K, 128).T)


def _units(W, KD):
    rows, ncols = W.shape
    assert rows == KD * 128
    return np.ascontiguousarray(W.reshape(KD, 128, ncols // 128, 128).transpose(2, 1, 0, 3)).reshape(ncols // 128, 128, KD * 128)


def prepare(cfg, inp):
    D, KD, KF, KH, depth = cfg.D, cfg.KD, cfg.KF, cfg.KH, cfg.depth
    f32 = np.float32
    units = []
    pps = np.zeros((depth, 128, cfg.ppw), f32)
    for i in range(depth):
        j = i // 2
        pr = pps[i]
        pr[:, cfg.o_mpre:cfg.o_mpre + KD] = _fm(inp["norm_mix_pre"][i], KD)
        pr[:, cfg.o_mpost:cfg.o_mpost + KD] = _fm(inp["norm_mix_post"][i], KD)
        pr[:, cfg.o_fpre:cfg.o_fpre + KD] = _fm(inp["norm_ffn_pre"][i], KD)
        pr[:, cfg.o_fpost:cfg.o_fpost + KD] = _fm(inp["norm_ffn_post"][i], KD)
        fw = np.asarray(inp["f_w_dw"][i], f32)
        pr[:, cfg.o_fw:cfg.o_fw + KF * CF] = fw.reshape(CF, KF, 128).transpose(2, 1, 0).reshape(128, KF * CF)
        pr[:, cfg.o_fb:cfg.o_fb + KF] = _fm(inp["f_b_dw"][i], KF)
        om = cfg.o_mix
        if i % 2 == 0:
            w_in = np.asarray(inp["a_w_in"][j], f32)
            u_in = _units(w_in, KD)
            units.append(u_in[KD:2 * KD])
            units.append(u_in[0:KD])
            units.append(_units(np.asarray(inp["a_w_out"][j], f32), KD))
            pr[:, om:om + 2 * KD] = _fm(inp["a_b_in"][j], 2 * KD)
            pr[:, om + 2 * KD:om + 3 * KD] = _fm(inp["a_ln_g"][j], KD)
            pr[:, om + 3 * KD:om + 4 * KD] = _fm(inp["a_ln_b"][j], KD)
        else:
            w_in = np.asarray(inp["b_w_in"][j], f32)
            u_in = _units(w_in, KD)
            inter = np.empty((2 * KD,) + u_in.shape[1:], f32)
            inter[0::2] = u_in[0:KD]
            inter[1::2] = u_in[KD:2 * KD]
            units.append(inter)
            units.append(_units(np.asarray(inp["b_w_out"][j], f32), KD))
            pr[:, om:om + 2 * KD] = _fm(inp["b_b_in"][j], 2 * KD)
            wd = np.asarray(inp["b_w_dw"][j], f32)
            pr[:, om + 2 * KD:om + 2 * KD + CB * KD] = wd.reshape(CB, KD, 128).transpose(2, 1, 0).reshape(128, KD * CB)
            o_b = om + 2 * KD + CB * KD
            pr[:, o_b:o_b + KD] = _fm(inp["b_b_dw"][j], KD)
            pr[:, o_b + KD:o_b + 2 * KD] = _fm(inp["b_ln_g"][j], KD)
            pr[:, o_b + 2 * KD:o_b + 3 * KD] = _fm(inp["b_ln_b"][j], KD)
        u_up = _units(np.asarray(inp["f_w_up"][i], f32), KD)
        inter = np.empty_like(u_up)
        inter[0::2] = u_up[0:KH]
        inter[1::2] = u_up[KH:2 * KH]
        units.append(inter)
        wdn = np.asarray(inp["f_w_down"][i], f32)
        d0 = _units(wdn[0:D], KD)
        d1 = _units(wdn[D:2 * D], KD)
        inter = np.empty((2 * KD,) + d0.shape[1:], f32)
        inter[0::2] = d0
        inter[1::2] = d1
        units.append(inter)
    ws = np.concatenate(units, axis=0)
    assert ws.shape[0] == cfg.NU, (ws.shape, cfg.NU)
    ws = ws.reshape(cfg.NU * 128, KD * 128)
    wst = np.ascontiguousarray(np.asarray(inp["a_w_s"], f32).transpose(0, 3, 1, 2)).reshape(cfg.NA * 128, GA * 128)
    bs = np.ascontiguousarray(np.asarray(inp["a_b_s"], f32)).reshape(cfg.NA, GA * 128)
    shared = dict(ws=ws, pp=pps.reshape(depth * 128, cfg.ppw), wst=wst, bs=bs,
                  ident=np.eye(128, dtype=f32))
    x_prompt = np.asarray(inp["x_prompt"], f32)
    x_sample = np.asarray(inp["x_sample"], f32)
    sc = np.asarray(inp["state_conv"], f32)
    sf = np.asarray(inp["state_ffn"], f32)
    per_seq = x_prompt.shape[1] // SHARD
    in_maps = []
    for c in range(NCORES):
        b, q = divmod(c, per_seq)
        s0 = q * SHARD
        xpc = np.zeros((HALO + SHARD, D), f32)
        lo = max(0, s0 - HALO)
        xpc[HALO - (s0 - lo):] = x_prompt[b, lo:s0 + SHARD]
        m = dict(shared)
        m["xp"] = xpc
        m["xs"] = np.ascontiguousarray(x_sample[c])
        m["valid"] = np.full((128, 1), 0.0 if q == 0 else 1.0, f32)
        m["sconv"] = np.ascontiguousarray(sc[:, c]).reshape(cfg.NB * (CB - 1), D)
        m["sffn"] = np.ascontiguousarray(sf[:, c]).reshape(depth * (CF - 1), 4 * D)
        in_maps.append(m)
    return in_maps


def assemble(cfg, res, batch, seq):
    D, depth = cfg.D, cfg.depth
    per_seq = seq // SHARD
    y_prompt = np.empty((batch, seq, D), np.float32)
    for c in range(NCORES):
        b, q = divmod(c, per_seq)
        y_prompt[b, q * SHARD:(q + 1) * SHARD] = res[c]["yp"]
    y_sample = np.stack([res[c]["ys"] for c in range(NCORES)])
    last = [b * per_seq + per_seq - 1 for b in range(batch)]
    ncp = np.stack([res[c]["ncp"].reshape(cfg.NB, CB - 1, D) for c in last], axis=1)
    nfp = np.stack([res[c]["nfp"].reshape(depth, CF - 1, 4 * D) for c in last], axis=1)
    ncs = np.stack([res[c]["ncs"].reshape(cfg.NB, CB - 1, D) for c in range(NCORES)], axis=1)
    nfs = np.stack([res[c]["nfs"].reshape(depth, CF - 1, 4 * D) for c in range(NCORES)], axis=1)
    nvs = np.stack([res[c]["nvs"].reshape(cfg.NA, NS, D) for c in range(NCORES)], axis=1)
    return (y_prompt, y_sample, ncp, nfp, ncs, nfs, nvs)


_CACHE = {}


def run(cfg, inputs):
    key = (cfg.D, cfg.depth)
    if key not in _CACHE:
        _CACHE[key] = build(cfg)
    nc = _CACHE[key]
    in_maps = prepare(cfg, inputs)
    res = run_bass_kernel_spmd(nc, in_maps, core_ids=list(range(NCORES)))
    xpr = np.asarray(inputs["x_prompt"])
    return assemble(cfg, res.results, xpr.shape[0], xpr.shape[1])


def kernel(**inputs):
    return run(Cfg(4096, 4), inputs)
```

```python
import numpy as np
from contextlib import ExitStack
import concourse.bass as bass
import concourse.mybir as mybir
from concourse.bass_utils import run_bass_kernel_spmd

F32 = mybir.dt.float32
BF16 = mybir.dt.bfloat16
AF = mybir.ActivationFunctionType
ALU = mybir.AluOpType
EPS = 1e-6
NCORES = 8
HALO = 256
SHARD = 1024
NS = 64
CB = 31
CF = 3
GA = 8
TMAX = 384
NW = 4
EPOCH = 8192
ENG = ("pe", "act", "dve", "pool", "sp")


class Cfg:
    def __init__(self, D=4096, depth=4):
        self.D = D
        self.KD = D // 128
        self.KF = 4 * self.KD
        self.KH = 2 * self.KD
        self.depth = depth
        self.NA = (depth + 1) // 2
        self.NB = depth // 2
        self.cpg = max(1, self.KD // GA)
        KD, KF = self.KD, self.KF
        o = 0
        self.o_mpre = o; o += KD
        self.o_mpost = o; o += KD
        self.o_fpre = o; o += KD
        self.o_fpost = o; o += KD
        self.o_fw = o; o += KF * CF
        self.o_fb = o; o += KF
        self.o_mix = o
        self.ppw = o + max(4 * KD, 2 * KD + CB * KD + 3 * KD)
        self.units_per_layer = 3 * KD + KF + 2 * KD
        self.NU = depth * self.units_per_layer
        self.tiles = [
            dict(T=320, segs=[("p", 0, 256, 0, None), ("s", 256, 64, 0, 0)]),
            dict(T=384, segs=[("p", 0, 384, 256, 0)]),
            dict(T=384, segs=[("p", 0, 384, 640, 384)]),
            dict(T=256, segs=[("p", 0, 256, 1024, 768)]),
        ]


class Prog:
    def __init__(self):
        self.ops = {e: [] for e in ENG}
        self.cnt = {e: 0 for e in ENG}
        self.seen = {e: {} for e in ENG}
        self.lastw = {}
        self.readers = {}
        self.dcnt = {}

    def _deps(self, eng, reads, writes):
        raw, other = set(), set()
        for r in reads:
            t = self.lastw.get(r)
            if t is not None:
                raw.add(t)
            if r == "ptb" or (isinstance(r, tuple) and r[0] in ("pm", "pst", "ptr")):
                rd = self.readers.get(r)
                if rd:
                    other.update(rd.values())
        for w in writes:
            t = self.lastw.get(w)
            if t is not None:
                other.add(t)
            rd = self.readers.get(w)
            if rd:
                other.update(rd.values())
        waits = {}
        seen = self.seen[eng]
        for t in raw | other:
            if t[0] == "e":
                src, seq = t[1], t[2]
                if src == eng and (eng == "pe" or t not in raw):
                    continue
                key = src
            else:
                key, seq = ("d", t[1]), t[2]
            if seen.get(key, -1) >= seq:
                continue
            if waits.get(key, -1) < seq:
                waits[key] = seq
        for k, v in waits.items():
            seen[k] = v
        return list(waits.items())

    def _register(self, tok, reads, writes):
        src = tok[1]
        for r in reads:
            self.readers.setdefault(r, {})[src] = tok
        for w in writes:
            self.lastw[w] = tok
            self.readers[w] = {}

    stop = False
    stop_at = None

    def mark(self, name):
        if self.stop_at is not None and name == self.stop_at:
            self.stop = True

    def op(self, eng, fn, reads=(), writes=()):
        if self.stop:
            return
        waits = self._deps(eng, reads, writes)
        seq = self.cnt[eng]
        self.cnt[eng] += 1
        self.ops[eng].append((waits, fn, None))
        self._register(("e", eng, seq), reads, writes)

    def dma(self, eng, fn, key, reads=(), writes=()):
        if self.stop:
            return
        waits = self._deps(eng, reads, writes)
        n = self.dcnt.get(key, 0) + 1
        self.dcnt[key] = n
        self.ops[eng].append((waits, fn, key))
        self._register(("d", key, n), reads, writes)


def build(cfg):
    D, KD, KF, KH = cfg.D, cfg.KD, cfg.KF, cfg.KH
    depth, NA, NB, cpg = cfg.depth, cfg.NA, cfg.NB, cfg.cpg
    HB, HF = CB - 1, CF - 1
    nc = bass.Bass("TRN2", target_bir_lowering=False)

    def din(name, shape):
        return nc.dram_tensor(name, shape, F32, kind="ExternalInput").ap()

    def dout(name, shape):
        return nc.dram_tensor(name, shape, F32, kind="ExternalOutput").ap()

    xp = din("xp", [HALO + SHARD, D])
    xs = din("xs", [NS, D])
    valid_d = din("valid", [128, 1])
    ident_d = din("ident", [128, 128])
    sconv = din("sconv", [NB * HB, D])
    sffn = din("sffn", [depth * HF, 4 * D])
    ws = din("ws", [cfg.NU * 128, KD * 128])
    pp_d = din("pp", [depth * 128, cfg.ppw])
    wst_d = din("wst", [NA * 128, GA * 128])
    bs_d = din("bs", [NA, GA * 128])
    upc = max(1, (240 << 20) // (128 * KD * 128 * 2))
    wcs = [nc.dram_tensor(f"wcache{i}", [min(upc, cfg.NU - i * upc) * 128, KD * 128], BF16, kind="Internal").ap()
           for i in range((cfg.NU + upc - 1) // upc)]
    yp = dout("yp", [SHARD, D])
    ys = dout("ys", [NS, D])
    ncp = dout("ncp", [NB * HB, D])
    nfp = dout("nfp", [depth * HF, 4 * D])
    ncs = dout("ncs", [NB * HB, D])
    nfs = dout("nfs", [depth * HF, 4 * D])
    nvs = dout("nvs", [NA * NS, D])

    P = Prog()
    P.stop_at = (getattr(cfg, "dbg", None) or {}).get("stop_at")
    EXTW = 416
    es = ExitStack()
    with es:
        def sb(name, shape, dt):
            return es.enter_context(nc.sbuf_tensor("sb_" + name, shape, dt))

        def psum(name, shape, dt):
            return es.enter_context(nc.psum_tensor("ps_" + name, shape, dt))

        xT = sb("xT", [128, KD, TMAX], F32)
        hT = sb("hT", [128, KD, TMAX], BF16)
        big_f = sb("big", [128, KD, TMAX], F32)
        big_b = sb_b = big_f.bitcast(BF16)
        wsl = sb("wsl", [128, NW, KD * 128], BF16)
        pp = sb("pp", [128, cfg.ppw], F32)
        Hg_p = sb("Hg_p", [128, NB * KD, HB], F32)
        Hg_s = sb("Hg_s", [128, KD, HB], F32)
        Hu_p = sb("Hu_p", [128, depth * KF, HF], F32)
        Hu_s = sb("Hu_s", [128, KF, HF], F32)
        ext = sb("ext", [128, 4, EXTW], F32)
        scr = sb("scr", [128, 9, TMAX], F32)
        scr_b = scr.bitcast(BF16)
        ident_f = sb("ident_f", [128, 128], F32)
        ident_b = sb("ident_b", [128, 128], BF16)
        ones_b = sb("ones_b", [128, 128], BF16)
        wst_b = sb("wst_b", [128, GA * 128], BF16)
        bs_b = sb("bs_b", [1, 2 * GA * 128], BF16)
        so_f = sb("so_f", [128, 2, 512], F32)
        so_flat = so_f[:].rearrange("p a b -> p (a b)")
        ext_flat = ext[:].rearrange("p a b -> p (a b)")
        valid = sb("valid_sb", [128, 1], F32)
        epsc = sb("epsc", [128, 1], F32)

        pm = psum("pm", [128, 3, 512], F32)
        pst = psum("pst", [128, 2, 512], F32)
        ptr = psum("ptr", [128, 2, 512], F32)
        ptb = psum("ptb", [128, 1, 1024], BF16)

        def bigb(j, c0, c1):
            return big_b[:, j // 2, (j % 2) * TMAX + c0:(j % 2) * TMAX + c1]

        hT_flat = hT[:].rearrange("p k t -> p (k t)")
        big_flat = big_f[:].rearrange("p k t -> p (k t)")

        def vtok(s, r0, r1, c0, c1):
            return hT_flat[r0:r1, s * D + c0:s * D + c1]

        def vtok_keys(s):
            lo = (s * D) // TMAX
            hi = ((s + 1) * D + TMAX - 1) // TMAX
            return [("h", k) for k in range(lo, min(hi, KD))]

        def stage(s, r0, r1, c0, c1):
            return big_flat[r0:r1, s * D + c0:s * D + c1]

        def stage_keys(s):
            lo = (s * D * 4) // (TMAX * 2)
            hi = ((s + 1) * D * 4 + TMAX * 2 - 1) // (TMAX * 2)
            return [("big", j) for j in range(lo, min(hi, 2 * KD))]

        NSTAGE = 2
        rot = {}

        def nxt(name, n):
            v = rot.get(name, 0)
            rot[name] = v + 1
            return v % n

        P.dma("sp", lambda e: e.dma_start(out=ident_f[:], in_=ident_d), "c_ident", writes=["ident_f"])
        P.dma("sp", lambda e: e.dma_start(out=valid[:], in_=valid_d), "c_valid", writes=["valid"])
        P.op("act", lambda e: e.copy(out=ident_b[:], in_=ident_f[:]), reads=["ident_f"], writes=["ident_b"])
        P.op("dve", lambda e: e.memset(ones_b[:], 1.0), writes=["ones_b"])
        P.op("dve", lambda e: e.memset(epsc[:], EPS), writes=["epsc"])
        P.op("dve", lambda e: e.memset(Hg_p[:], 0.0), writes=[("Hg_p", j, c) for j in range(NB) for c in range(KD)])
        P.op("dve", lambda e: e.memset(Hu_p[:], 0.0), writes=[("Hu_p", i, c) for i in range(depth) for c in range(KF)])

        wstate = dict(next_dma=0, next_use=0)

        def w_dma(u):
            slot = u % NW
            pas, uu = divmod(u, cfg.NU)
            early = (uu % 3 == 0)
            rows = slice(uu * 128, (uu + 1) * 128)
            wc = wcs[uu // upc]
            crow = slice((uu % upc) * 128, (uu % upc + 1) * 128)
            from_cache = (pas >= 2) or (pas == 1 and early)
            if from_cache:
                P.dma("pool", lambda e: e.dma_start(out=wsl[:, slot, :], in_=wc[crow, :]),
                      ("w", slot), reads=[("wc", uu)], writes=[("w", slot)])
                return
            P.dma("pool", lambda e: e.dma_start(out=wsl[:, slot, :], in_=ws[rows, :]),
                  ("w", slot), writes=[("w", slot)])
            if len(cfg.tiles) > 2 and ((pas == 0 and early) or pas == 1):
                P.dma("sp", lambda e: e.dma_start(out=wc[crow, :], in_=wsl[:, slot, :]),
                      ("wb", slot), reads=[("w", slot)], writes=[("wc", uu)])

        def mm_unit(bank, T, rhs_fn, rhs_keys, start=True, stop=True):
            u = wstate["next_use"]
            wstate["next_use"] = u + 1
            while wstate["next_dma"] <= min(u + NW - 1, total_units - 1):
                w_dma(wstate["next_dma"])
                wstate["next_dma"] += 1
            slot = u % NW

            def fn(e):
                ins = None
                for k in range(KD):
                    ins = e.matmul(pm[:, bank, 0:T], wsl[:, slot, k * 128:(k + 1) * 128], rhs_fn(k),
                                   start=(start and k == 0), stop=(stop and k == KD - 1))
                return ins
            P.op("pe", fn, reads=[("w", slot)] + rhs_keys, writes=[("pm", bank)])

        total_units = cfg.NU * len(cfg.tiles)

        def pcol(off, n=1):
            return pp[:, off:off + n]

        def load_pp(layer):
            P.dma("sp", lambda e: e.dma_start(out=pp[:], in_=pp_d[layer * 128:(layer + 1) * 128, :]),
                  "pp", writes=["pp"])

        def stat_finish_rms(T, dst):
            P.op("act", lambda e: e.activation(out=scr[:, dst, 0:T], in_=pst[:, 0, 0:T], func=AF.Sqrt, bias=epsc[:, 0:1], scale=1.0 / D),
                 reads=[("pst", 0), "epsc"], writes=[("scr", dst)])
            P.op("dve", lambda e: e.reciprocal(out=scr[:, dst, 0:T], in_=scr[:, dst, 0:T]),
                 reads=[("scr", dst)], writes=[("scr", dst)])

        def rms_in(T, goff):
            for k in range(KD):
                s = 6 + nxt("sq", 2)
                P.op("act", lambda e, k=k, s=s: e.activation(out=scr_b[:, s, 0:T], in_=xT[:, k, 0:T], func=AF.Square),
                     reads=[("x", k)], writes=[("scr", s)])
                P.op("pe", lambda e, k=k, s=s: e.matmul(pst[:, 0, 0:T], ones_b[:, :], scr_b[:, s, 0:T],
                                                        start=(k == 0), stop=(k == KD - 1)),
                     reads=[("scr", s), "ones_b"], writes=[("pst", 0)])
            stat_finish_rms(T, 8)
            for k in range(KD):
                P.op("dve", lambda e, k=k: e.scalar_tensor_tensor(
                    out=hT[:, k, 0:T], in0=xT[:, k, 0:T], scalar=pcol(goff + k), in1=scr[:, 8, 0:T],
                    op0=ALU.mult, op1=ALU.mult),
                    reads=[("x", k), ("scr", 8), "pp"], writes=[("h", k)])

        pend = []

        def flush_pend(keep=0):
            while len(pend) > keep:
                pend.pop(0)()

        def post_evac(bank, T, j):
            s = 6 + nxt("sq", 2)
            P.op("act", lambda e: e.activation(out=scr_b[:, s, 0:T], in_=pm[:, bank, 0:T], func=AF.Square),
                 reads=[("pm", bank)], writes=[("scr", s)])
            P.op("dve", lambda e: e.tensor_copy(out=hT[:, j, 0:T], in_=pm[:, bank, 0:T]),
                 reads=[("pm", bank)], writes=[("h", j)])
            pend.append(lambda: P.op("pe", lambda e: e.matmul(pst[:, 0, 0:T], ones_b[:, :], scr_b[:, s, 0:T],
                                                              start=(j == 0), stop=(j == KD - 1)),
                                     reads=[("scr", s), "ones_b"], writes=[("pst", 0)]))

        def post_finish(T, goff):
            flush_pend()
            stat_finish_rms(T, 8)
            def op2(j, s):
                P.op("dve", lambda e: e.tensor_tensor(
                    out=xT[:, j, 0:T], in0=xT[:, j, 0:T], in1=scr[:, s, 0:T], op=ALU.add),
                    reads=[("x", j), ("scr", s)], writes=[("x", j)])
            prev = None
            for j in range(KD):
                s = nxt("tmpx", 2)
                P.op("dve", lambda e, j=j, s=s: e.scalar_tensor_tensor(
                    out=scr[:, s, 0:T], in0=hT[:, j, 0:T], scalar=pcol(goff + j), in1=scr[:, 8, 0:T],
                    op0=ALU.mult, op1=ALU.mult),
                    reads=[("h", j), ("scr", 8), "pp"], writes=[("scr", s)])
                if prev is not None:
                    op2(*prev)
                prev = (j, s)
            op2(*prev)

        def ln_stats_finish(T, sc):
            P.op("dve", lambda e: e.tensor_scalar(out=scr[:, 8, 0:T], in0=pst[:, 0, 0:T], scalar1=sc,
                                                  scalar2=None, op0=ALU.mult),
                 reads=[("pst", 0)], writes=[("scr", 8)])
            P.op("dve", lambda e: e.tensor_tensor(out=scr[:, 6, 0:T], in0=scr[:, 8, 0:T], in1=scr[:, 8, 0:T],
                                                  op=ALU.mult),
                 reads=[("scr", 8)], writes=[("scr", 6)])
            P.op("dve", lambda e: e.scalar_tensor_tensor(out=scr[:, 7, 0:T], in0=pst[:, 1, 0:T], scalar=sc,
                                                         in1=scr[:, 6, 0:T], op0=ALU.mult, op1=ALU.subtract),
                 reads=[("pst", 1), ("scr", 6)], writes=[("scr", 7)])
            P.op("act", lambda e: e.activation(out=scr[:, 7, 0:T], in_=scr[:, 7, 0:T], func=AF.Sqrt, bias=epsc[:, 0:1]),
                 reads=[("scr", 7), "epsc"], writes=[("scr", 7)])
            P.op("dve", lambda e: e.reciprocal(out=scr[:, 7, 0:T], in_=scr[:, 7, 0:T]),
                 reads=[("scr", 7)], writes=[("scr", 7)])
            P.op("dve", lambda e: e.scalar_tensor_tensor(out=scr[:, 8, 0:T], in0=scr[:, 8, 0:T], scalar=-1.0,
                                                         in1=scr[:, 7, 0:T], op0=ALU.mult, op1=ALU.mult),
                 reads=[("scr", 8), ("scr", 7)], writes=[("scr", 8)])

        def out_rows_T(src_fn, src_keys, nchunks, nrows, dst, dst_row0):
            for r0 in range(0, nrows, 4):
                nr = min(4, nrows - r0)
                b = nxt("ptr", 2)
                s = nxt("so", 2)

                def ft(e, nr=nr, r0=r0, b=b):
                    ins = None
                    for rr in range(nr):
                        ins = e.transpose(ptr[0:nchunks, b, rr * 128:(rr + 1) * 128], src_fn(r0 + rr),
                                          ident_f[:, :])
                    return ins
                P.op("pe", ft, reads=src_keys + ["ident_f"], writes=[("ptr", b)])
                P.op("act", lambda e, nr=nr, b=b, s=s: e.copy(out=so_f[0:nchunks, s, 0:nr * 128], in_=ptr[0:nchunks, b, 0:nr * 128]),
                     reads=[("ptr", b)], writes=[("so", s)])
                P.dma("sp", lambda e, nr=nr, r0=r0, s=s: e.dma_start(
                    out=dst[dst_row0 + r0:dst_row0 + r0 + nr, :].rearrange("r (c p) -> c r p", p=128),
                    in_=so_f[0:nchunks, s, 0:nr * 128].rearrange("c (r p) -> c r p", p=128)),
                    ("so", s), reads=[("so", s)])

        def in_rows_T(src, src_row0, nrows, nchunks, dst_fn, dst_keys):
            for c0 in range(0, nchunks, 4):
                b = nxt("ptr", 2)
                s = nxt("so", 2)
                P.dma("sp", lambda e, s=s, c0=c0: e.dma_start(out=so_f[0:nrows, s, :],
                                                   in_=src[src_row0:src_row0 + nrows, c0 * 128:(c0 + 4) * 128]),
                      ("so", s), writes=[("so", s)])

                def ft(e, s=s, b=b):
                    ins = None
                    for q in range(4):
                        ins = e.transpose(ptr[:, b, q * 128:q * 128 + nrows], so_f[0:nrows, s, q * 128:(q + 1) * 128],
                                          ident_f[0:nrows, 0:nrows])
                    return ins
                P.op("pe", ft, reads=[("so", s), "ident_f"], writes=[("ptr", b)])
                P.op("act", lambda e, c0=c0, b=b: e.copy(
                    out=dst_fn(c0, 4),
                    in_=ptr[:, b, :].rearrange("p (q r) -> p q r", r=128)[:, :, 0:nrows]),
                    reads=[("ptr", b)], writes=[(dst_keys, c0 + q) for q in range(4)])

        def tile_load(tile):
            for (kind, col0, ln, src0, own0) in tile["segs"]:
                src = xp if kind == "p" else xs
                for r0 in range(0, ln, 128):
                    rows = min(128, ln - r0)
                    s = nxt("stage", NSTAGE)
                    P.dma("sp", lambda e, s=s, rows=rows, r0=r0, src=src, src0=src0: e.dma_start(
                        out=stage(s, 0, rows, 0, D), in_=src[src0 + r0:src0 + r0 + rows, :]),
                        ("stage", s), writes=stage_keys(s))
                    for k0 in range(0, KD, 4):
                        b = nxt("ptr", 2)

                        def ft(e, s=s, rows=rows, k0=k0, b=b):
                            ins = None
                            for q in range(4):
                                ins = e.transpose(ptr[:, b, q * 128:q * 128 + rows],
                                                  stage(s, 0, rows, (k0 + q) * 128, (k0 + q + 1) * 128),
                                                  ident_f[0:rows, 0:rows])
                            return ins
                        P.op("pe", ft, reads=stage_keys(s) + ["ident_f"], writes=[("ptr", b)])
                        P.op("act", lambda e, rows=rows, k0=k0, b=b, c=col0 + r0: e.copy(
                            out=xT[:, k0:k0 + 4, c:c + rows],
                            in_=ptr[:, b, :].rearrange("p (q r) -> p q r", r=128)[:, :, 0:rows]),
                            reads=[("ptr", b)], writes=[("x", k0 + q) for q in range(4)])

        def tile_store(tile):
            for (kind, col0, ln, src0, own0) in tile["segs"]:
                if own0 is None:
                    continue
                dst = yp if kind == "p" else ys
                for r0 in range(0, ln, 128):
                    rows = min(128, ln - r0)
                    s = nxt("stage", NSTAGE)
                    for k0 in range(0, KD, 4):
                        b = nxt("ptr", 2)

                        def ft(e, rows=rows, k0=k0, b=b, c=col0 + r0):
                            ins = None
                            for q in range(4):
                                ins = e.transpose(ptr[0:rows, b, q * 128:(q + 1) * 128],
                                                  xT[:, k0 + q, c:c + rows], ident_f[:, :])
                            return ins
                        P.op("pe", ft, reads=[("x", k0 + q) for q in range(4)] + ["ident_f"], writes=[("ptr", b)])
                        P.op("act", lambda e, s=s, rows=rows, k0=k0, b=b: e.copy(
                            out=stage(s, 0, rows, k0 * 128, (k0 + 4) * 128), in_=ptr[0:rows, b, :]),
                            reads=[("ptr", b)], writes=stage_keys(s))
                    P.dma("sp", lambda e, s=s, rows=rows, dst=dst, o=own0 + r0: e.dma_start(
                        out=dst[o:o + rows, :], in_=stage(s, 0, rows, 0, D)),
                        ("stage", s), reads=stage_keys(s))

        def ext_layout(tile, H):
            segs = []
            o = 0
            for (kind, col0, ln, src0, own0) in tile["segs"]:
                segs.append((kind, col0, ln, o))
                o += H + ln
            return segs, o

        def ffn(ti, tile, layer):
            T = tile["T"]
            last_p = (ti == len(cfg.tiles) - 1)
            P.mark("ffn.start")
            rms_in(T, cfg.o_fpre)
            P.mark("ffn.rms")
            segs, W = ext_layout(tile, HF)
            if ti == 0:
                in_rows_T(sffn, layer * HF, HF, KF, lambda c0, n: Hu_s[:, c0:c0 + n, :], "Hu_s")
            P.mark("ffn.inrows")
            for c in range(KH):
                par = c % 2
                if c == 1:
                    P.mark("ffn.pair0")
                for half in range(2):
                    cc = c + half * KH
                    bank = nxt("pm", 3)
                    mm_unit(bank, T, lambda k: hT[:, k, 0:T], [("h", k) for k in range(KD)])
                    eb = 2 * par + half
                    cv = 2 * half + par
                    for (kind, col0, ln, eo) in segs:
                        if kind == "p":
                            Hsrc = Hu_p[:, layer * KF + cc, :]
                            hk = ("Hu_p", layer, cc)
                        else:
                            Hsrc = Hu_s[:, cc, :]
                            hk = ("Hu_s", cc)
                        P.op("dve", lambda e, eb=eb, eo=eo, Hsrc=Hsrc: e.tensor_copy(out=ext[:, eb, eo:eo + HF], in_=Hsrc),
                             reads=[hk], writes=[("ext", eb)])
                        P.op("act", lambda e, eb=eb, eo=eo, ln=ln, col0=col0, bank=bank: e.copy(
                            out=ext[:, eb, eo + HF:eo + HF + ln], in_=pm[:, bank, col0:col0 + ln]),
                            reads=[("pm", bank)], writes=[("ext", eb)])
                        if kind == "p" and ti == 0:
                            P.op("dve", lambda e, eb=eb, eo=eo, ln=ln, Hsrc=Hsrc: e.tensor_scalar(
                                out=Hsrc, in0=ext[:, eb, eo + ln:eo + ln + HF], scalar1=valid[:, 0:1], scalar2=None,
                                op0=ALU.mult), reads=[("ext", eb), "valid"], writes=[hk])
                        else:
                            P.op("dve", lambda e, eb=eb, eo=eo, ln=ln, Hsrc=Hsrc: e.tensor_copy(
                                out=Hsrc, in_=ext[:, eb, eo + ln:eo + ln + HF]), reads=[("ext", eb)], writes=[hk])
                    wo = cfg.o_fw + cc * CF
                    P.op("act", lambda e, eb=eb, cv=cv, wo=wo, cc=cc: e.activation(
                        out=scr[:, cv, 0:W - HF], in_=ext[:, eb, HF:W], func=AF.Identity,
                        scale=pcol(wo + 2), bias=pcol(cfg.o_fb + cc)),
                        reads=[("ext", eb), "pp"], writes=[("scr", cv)])
                    for k in (1, 0):
                        P.op("dve", lambda e, eb=eb, cv=cv, wo=wo, k=k: e.scalar_tensor_tensor(
                            out=scr[:, cv, 0:W - HF], in0=ext[:, eb, k:k + W - HF], scalar=pcol(wo + k),
                            in1=scr[:, cv, 0:W - HF], op0=ALU.mult, op1=ALU.add),
                            reads=[("ext", eb), ("scr", cv), "pp"], writes=[("scr", cv)])
                cg, cvv = par, 2 + par
                P.op("act", lambda e, cg=cg: e.activation(out=scr[:, cg, 0:W - HF], in_=scr[:, cg, 0:W - HF], func=AF.Silu),
                     reads=[("scr", cg)], writes=[("scr", cg)])
                for (kind, col0, ln, eo) in segs:
                    P.op("dve", lambda e, cg=cg, cvv=cvv, c=c, col0=col0, ln=ln, eo=eo: e.tensor_tensor(
                        out=bigb(c, col0, col0 + ln), in0=scr[:, cg, eo:eo + ln], in1=scr[:, cvv, eo:eo + ln],
                        op=ALU.mult), reads=[("scr", cg), ("scr", cvv)], writes=[("big", c)])
            P.mark("ffn.up")
            if ti == 0:
                out_rows_T(lambda r: Hu_s[:, :, r], [("Hu_s", c) for c in range(KF)], KF, HF, nfs, layer * HF)
            if last_p:
                out_rows_T(lambda r: Hu_p[:, layer * KF:(layer + 1) * KF, r], [("Hu_p", layer, c) for c in range(KF)], KF, HF, nfp, layer * HF)
            P.mark("ffn.outrows")
            for j in range(KD):
                bank = nxt("pm", 3)
                for half in range(2):
                    mm_unit(bank, T, lambda k, half=half: bigb(half * KD + k, 0, T),
                            [("big", half * KD + k) for k in range(KD)], start=(half == 0), stop=(half == 1))
                flush_pend()
                post_evac(bank, T, j)
            P.mark("ffn.down")
            post_finish(T, cfg.o_fpost)
            P.mark("ffn.end")

        def gmlp_blocks(tile):
            blocks = []
            for (kind, col0, ln, src0, own0) in tile["segs"]:
                for r0 in range(0, ln, 128):
                    blocks.append((col0 + r0, min(128, ln - r0), kind))
            return blocks

        def gmlp(ti, tile, layer):
            T = tile["T"]
            ja = layer // 2
            om = cfg.o_mix
            o_bin, o_lng, o_lnb = om, om + 2 * KD, om + 3 * KD
            SO = [("so", 0), ("so", 1)]
            EX = [("ext", 0), ("ext", 1), ("ext", 2)]
            P.dma("sp", lambda e: e.dma_start(out=so_flat[:, :], in_=wst_d[ja * 128:(ja + 1) * 128, :]), "wst",
                  writes=SO)
            P.op("act", lambda e: e.copy(out=wst_b[:], in_=so_flat[:, :]), reads=SO, writes=["wst_b"])
            wv = wst_b[:].rearrange("p (g i) -> p g i", i=128)
            P.op("dve", lambda e: e.memset(wv[64:128, :, 0:64], 0.0), reads=["wst_b"], writes=["wst_b"])
            P.dma("sp", lambda e: e.dma_start(out=ext_flat[0:1, 0:GA * 128], in_=bs_d[ja:ja + 1, :]), "bsd", writes=EX)
            P.op("act", lambda e: e.copy(out=bs_b[:, 0:GA * 128], in_=ext_flat[0:1, 0:GA * 128]), reads=EX, writes=["bs_b"])
            P.op("act", lambda e: e.copy(out=so_flat[0:1, :], in_=bs_b[:, 0:GA * 128]), reads=["bs_b"], writes=SO)
            P.op("dve", lambda e: e.tensor_tensor(out=bs_b[:, GA * 128:2 * GA * 128], in0=ext_flat[0:1, 0:GA * 128],
                                                  in1=so_flat[0:1, :], op=ALU.subtract),
                 reads=EX + SO + ["bs_b"], writes=["bs_b"])
            rms_in(T, cfg.o_mpre)
            hk = [("h", k) for k in range(KD)]
            for c in range(KD):
                bank = nxt("pm", 3)
                mm_unit(bank, T, lambda k: hT[:, k, 0:T], hk)
                flush_pend()
                P.op("act", lambda e, c=c, bank=bank: e.activation(
                    out=bigb(KD + c, 0, T), in_=pm[:, bank, 0:T], func=AF.Gelu, bias=pcol(o_bin + KD + c)),
                    reads=[("pm", bank), "pp"], writes=[("big", KD + c)])
                s = 4 + nxt("sqb", 2)
                P.op("act", lambda e, c=c, s=s: e.activation(out=scr_b[:, s, 0:T], in_=bigb(KD + c, 0, T), func=AF.Square),
                     reads=[("big", KD + c)], writes=[("scr", s)])

                def vstats(c=c, s=s):
                    P.op("pe", lambda e: e.matmul(pst[:, 0, 0:T], ones_b[:, :], bigb(KD + c, 0, T),
                                                  start=(c == 0), stop=(c == KD - 1)),
                         reads=[("big", KD + c), "ones_b"], writes=[("pst", 0)])
                    P.op("pe", lambda e: e.matmul(pst[:, 1, 0:T], ones_b[:, :], scr_b[:, s, 0:T],
                                                  start=(c == 0), stop=(c == KD - 1)),
                         reads=[("scr", s), "ones_b"], writes=[("pst", 1)])
                pend.append(vstats)
            flush_pend()
            ln_stats_finish(T, 1.0 / D)
            for c in range(KD):
                bank = nxt("pm", 3)
                mm_unit(bank, T, lambda k: hT[:, k, 0:T], hk)
                P.op("act", lambda e, c=c, bank=bank: e.activation(
                    out=bigb(c, 0, T), in_=pm[:, bank, 0:T], func=AF.Gelu, bias=pcol(o_bin + c)),
                    reads=[("pm", bank), "pp"], writes=[("big", c)])
                s = nxt("nrm", 2)
                P.op("dve", lambda e, c=c, s=s: e.tensor_tensor(out=scr[:, s, 0:T], in0=bigb(KD + c, 0, T),
                                                                in1=scr[:, 7, 0:T], op=ALU.mult),
                     reads=[("big", KD + c), ("scr", 7)], writes=[("scr", s)])
                P.op("dve", lambda e, s=s: e.tensor_tensor(out=scr[:, s, 0:T], in0=scr[:, s, 0:T], in1=scr[:, 8, 0:T],
                                                           op=ALU.add),
                     reads=[("scr", s), ("scr", 8)], writes=[("scr", s)])
                P.op("act", lambda e, c=c, s=s: e.activation(
                    out=bigb(KD + c, 0, T), in_=scr[:, s, 0:T], func=AF.Identity,
                    scale=pcol(o_lng + c), bias=pcol(o_lnb + c)),
                    reads=[("scr", s), "pp"], writes=[("big", KD + c)])
            for (bc, rows, kind) in gmlp_blocks(tile):
                vs = nxt("vtok", 3)
                vk = vtok_keys(vs)
                for c0 in range(0, KD, 8):
                    def ft(e, c0=c0, bc=bc, rows=rows):
                        ins = None
                        for q in range(8):
                            ins = e.transpose(ptb[0:rows, 0, q * 128:(q + 1) * 128],
                                              bigb(KD + c0 + q, bc, bc + rows), ident_b[:, :])
                        return ins
                    P.op("pe", ft, reads=[("big", KD + c0 + q) for q in range(8)] + ["ident_b"], writes=["ptb"])
                    P.op("act", lambda e, c0=c0, rows=rows, vs=vs: e.copy(
                        out=vtok(vs, 0, rows, c0 * 128, (c0 + 8) * 128), in_=ptb[0:rows, 0, :]),
                        reads=["ptb"], writes=vk)
                if kind == "s":
                    for g0 in range(0, D, 512):
                        s = nxt("so", 2)
                        P.op("act", lambda e, s=s, g0=g0, vs=vs, rows=rows: e.copy(
                            out=so_f[0:rows, s, :], in_=vtok(vs, 0, rows, g0, g0 + 512)),
                            reads=vk, writes=[("so", s)])
                        P.dma("sp", lambda e, s=s, g0=g0, rows=rows: e.dma_start(
                            out=nvs[ja * NS:ja * NS + rows, g0:g0 + 512], in_=so_f[0:rows, s, :]),
                            ("so", s), reads=[("so", s)])
                for c0 in range(0, KD, 4):
                    b = nxt("ptr", 2)

                    def fm(e, c0=c0, b=b, rows=rows, vs=vs):
                        ins = None
                        for q in range(4):
                            c = c0 + q
                            g = c // cpg
                            o = ptr[:, b, q * 128:q * 128 + rows]
                            e.matmul(o, vtok(vs, 0, rows, c * 128, (c + 1) * 128),
                                     wst_b[0:rows, g * 128:g * 128 + rows], start=True, stop=False)
                            e.matmul(o, ones_b[0:1, :], bs_b[0:1, g * 128:g * 128 + rows], start=False, stop=False)
                            ins = e.matmul(o, ones_b[0:1, :], bs_b[0:1, (GA + g) * 128:(GA + g) * 128 + rows],
                                           start=False, stop=True)
                        return ins
                    P.op("pe", fm, reads=vk + ["wst_b", "bs_b", "ones_b"], writes=[("ptr", b)])
                    P.op("dve", lambda e, c0=c0, b=b, rows=rows, bc=bc: e.tensor_tensor(
                        out=big_b[:, c0 // 2:c0 // 2 + 2, :].rearrange("p a (h t) -> p (a h) t", h=2)[:, :, bc:bc + rows],
                        in0=big_b[:, c0 // 2:c0 // 2 + 2, :].rearrange("p a (h t) -> p (a h) t", h=2)[:, :, bc:bc + rows],
                        in1=ptr[:, b, :].rearrange("p (q r) -> p q r", r=128)[:, :, 0:rows], op=ALU.mult),
                        reads=[("ptr", b)] + [("big", c0 + q) for q in range(4)],
                        writes=[("big", c0 + q) for q in range(4)])
            for j in range(KD):
                bank = nxt("pm", 3)
                mm_unit(bank, T, lambda k: bigb(k, 0, T), [("big", k) for k in range(KD)])
                flush_pend()
                post_evac(bank, T, j)
            post_finish(T, cfg.o_mpost)

        def conformer(ti, tile, layer):
            T = tile["T"]
            jb = layer // 2
            last_p = (ti == len(cfg.tiles) - 1)
            om = cfg.o_mix
            o_bin, o_w, o_b = om, om + 2 * KD, om + 2 * KD + CB * KD
            o_lng, o_lnb = o_b + KD, o_b + 2 * KD
            rms_in(T, cfg.o_mpre)
            segs, W = ext_layout(tile, HB)
            if ti == 0:
                in_rows_T(sconv, jb * HB, HB, KD, lambda c0, n: Hg_s[:, c0:c0 + n, :], "Hg_s")
            hk = [("h", k) for k in range(KD)]
            for c in range(KD):
                bA = nxt("pm", 3)
                mm_unit(bA, T, lambda k: hT[:, k, 0:T], hk)
                bB = nxt("pm", 3)
                mm_unit(bB, T, lambda k: hT[:, k, 0:T], hk)
                flush_pend(keep=2)
                sg = 4 + nxt("sig", 2)
                P.op("act", lambda e, c=c, bB=bB, sg=sg: e.activation(
                    out=scr[:, sg, 0:T], in_=pm[:, bB, 0:T], func=AF.Sigmoid, bias=pcol(o_bin + KD + c)),
                    reads=[("pm", bB), "pp"], writes=[("scr", sg)])
                eb = nxt("gext", 2)
                for (kind, col0, ln, eo) in segs:
                    if kind == "p":
                        Hsrc = Hg_p[:, jb * KD + c, :]
                        hkey = ("Hg_p", jb, c)
                    else:
                        Hsrc = Hg_s[:, c, :]
                        hkey = ("Hg_s", c)
                    P.op("act", lambda e, eb=eb, eo=eo, Hsrc=Hsrc: e.copy(out=ext[:, eb, eo:eo + HB], in_=Hsrc),
                         reads=[hkey], writes=[("ext", eb)])
                    P.op("dve", lambda e, eb=eb, eo=eo, ln=ln, col0=col0, bA=bA, sg=sg, c=c: e.scalar_tensor_tensor(
                        out=ext[:, eb, eo + HB:eo + HB + ln], in0=pm[:, bA, col0:col0 + ln], scalar=pcol(o_bin + c),
                        in1=scr[:, sg, col0:col0 + ln], op0=ALU.add, op1=ALU.mult),
                        reads=[("pm", bA), ("scr", sg), "pp"], writes=[("ext", eb)])
                    if kind == "p" and ti == 0:
                        P.op("dve", lambda e, eb=eb, eo=eo, ln=ln, Hsrc=Hsrc: e.tensor_scalar(
                            out=Hsrc, in0=ext[:, eb, eo + ln:eo + ln + HB], scalar1=valid[:, 0:1], scalar2=None,
                            op0=ALU.mult), reads=[("ext", eb), "valid"], writes=[hkey])
                    else:
                        P.op("dve", lambda e, eb=eb, eo=eo, ln=ln, Hsrc=Hsrc: e.tensor_copy(
                            out=Hsrc, in_=ext[:, eb, eo + ln:eo + ln + HB]), reads=[("ext", eb)], writes=[hkey])
                i2 = nxt("acc", 2)
                ac, ab = 2 + i2, i2
                Wc = W - HB
                P.op("act", lambda e, eb=eb, ac=ac, c=c: e.activation(
                    out=scr[:, ac, 0:Wc], in_=ext[:, eb, HB:W], func=AF.Identity,
                    scale=pcol(o_w + c * CB + HB), bias=pcol(o_b + c)),
                    reads=[("ext", eb), "pp"], writes=[("scr", ac)])
                P.op("dve", lambda e, eb=eb, ab=ab, c=c: e.tensor_scalar(
                    out=scr[:, ab, 0:Wc], in0=ext[:, eb, 0:Wc], scalar1=pcol(o_w + c * CB), scalar2=None, op0=ALU.mult),
                    reads=[("ext", eb), "pp"], writes=[("scr", ab)])
                for k in range(1, HB):
                    tg = ac if k % 2 == 1 else ab
                    P.op("dve", lambda e, eb=eb, tg=tg, c=c, k=k: e.scalar_tensor_tensor(
                        out=scr[:, tg, 0:Wc], in0=ext[:, eb, k:k + Wc], scalar=pcol(o_w + c * CB + k),
                        in1=scr[:, tg, 0:Wc], op0=ALU.mult, op1=ALU.add),
                        reads=[("ext", eb), ("scr", tg), "pp"], writes=[("scr", tg)])
                for (kind, col0, ln, eo) in segs:
                    P.op("dve", lambda e, ac=ac, ab=ab, c=c, eo=eo, ln=ln, col0=col0: e.tensor_tensor(
                        out=big_f[:, c, col0:col0 + ln], in0=scr[:, ac, eo:eo + ln], in1=scr[:, ab, eo:eo + ln],
                        op=ALU.add),
                        reads=[("scr", ac), ("scr", ab)], writes=[("big", 2 * c), ("big", 2 * c + 1)])
                def cstats(c=c):
                    s = 6 + nxt("sq", 2)
                    P.op("act", lambda e: e.copy(out=scr_b[:, s, 0:T], in_=big_f[:, c, 0:T]),
                         reads=[("big", 2 * c), ("big", 2 * c + 1)], writes=[("scr", s)])
                    P.op("pe", lambda e: e.matmul(pst[:, 0, 0:T], ones_b[:, :], scr_b[:, s, 0:T],
                                                  start=(c == 0), stop=(c == KD - 1)),
                         reads=[("scr", s), "ones_b"], writes=[("pst", 0)])
                    s2 = 6 + nxt("sq", 2)
                    P.op("act", lambda e: e.activation(out=scr_b[:, s2, 0:T], in_=big_f[:, c, 0:T], func=AF.Square),
                         reads=[("big", 2 * c), ("big", 2 * c + 1)], writes=[("scr", s2)])
                    P.op("pe", lambda e: e.matmul(pst[:, 1, 0:T], ones_b[:, :], scr_b[:, s2, 0:T],
                                                  start=(c == 0), stop=(c == KD - 1)),
                         reads=[("scr", s2), "ones_b"], writes=[("pst", 1)])
                pend.append(cstats)
            flush_pend()
            if ti == 0:
                out_rows_T(lambda r: Hg_s[:, :, r], [("Hg_s", c) for c in range(KD)], KD, HB, ncs, jb * HB)
            if last_p:
                out_rows_T(lambda r: Hg_p[:, jb * KD:(jb + 1) * KD, r], [("Hg_p", jb, c) for c in range(KD)], KD, HB, ncp, jb * HB)
            ln_stats_finish(T, 1.0 / D)
            for c in range(KD):
                s = nxt("nrm", 2)
                P.op("dve", lambda e, c=c, s=s: e.tensor_tensor(out=scr[:, s, 0:T], in0=big_f[:, c, 0:T],
                                                                in1=scr[:, 7, 0:T], op=ALU.mult),
                     reads=[("big", 2 * c), ("big", 2 * c + 1), ("scr", 7)], writes=[("scr", s)])
                P.op("dve", lambda e, s=s: e.tensor_tensor(out=scr[:, s, 0:T], in0=scr[:, s, 0:T], in1=scr[:, 8, 0:T],
                                                           op=ALU.add),
                     reads=[("scr", s), ("scr", 8)], writes=[("scr", s)])
                P.op("act", lambda e, c=c, s=s: e.activation(
                    out=bigb(c, 0, T), in_=scr[:, s, 0:T], func=AF.Silu, scale=pcol(o_lng + c), bias=pcol(o_lnb + c)),
                    reads=[("scr", s), "pp"], writes=[("big", c)])
            for j in range(KD):
                bank = nxt("pm", 3)
                mm_unit(bank, T, lambda k: bigb(k, 0, T), [("big", k) for k in range(KD)])
                flush_pend()
                post_evac(bank, T, j)
            post_finish(T, cfg.o_mpost)

        dbg = getattr(cfg, "dbg", None) or {}
        for ti, tile in enumerate(cfg.tiles[:dbg.get("ntiles", 99)]):
            tile_load(tile)
            for layer in range(min(depth, dbg.get("nlayers", 99))):
                load_pp(layer)
                if dbg.get("mixer", True):
                    if layer % 2 == 0:
                        gmlp(ti, tile, layer)
                    else:
                        conformer(ti, tile, layer)
                if dbg.get("ffn", True):
                    ffn(ti, tile, layer)
            tile_store(tile)
        if not dbg:
            assert wstate["next_use"] == total_units, (wstate, total_units)

        n_ep = {e: (P.cnt[e] + EPOCH - 1) // EPOCH for e in ENG}
        esem = {e: [es.enter_context(nc.semaphore(f"s_{e}{i}")) for i in range(n_ep[e])] for e in ENG}
        dsem = {k: es.enter_context(nc.semaphore("d_" + "_".join(str(x) for x in (k if isinstance(k, tuple) else (k,)))))
                for k in P.dcnt}
        fin = es.enter_context(nc.semaphore("fin"))

        def emit(eng_name, e):
            seq = 0
            for (waits, fn, dkey) in P.ops[eng_name]:
                for (key, val) in waits:
                    if isinstance(key, tuple) and key[0] == "d":
                        e.wait_ge(dsem[key[1]], 16 * val)
                    else:
                        e.wait_ge(esem[key][val // EPOCH], val % EPOCH + 1)
                ins = fn(e)
                if dkey is None:
                    ins.then_inc(esem[eng_name][seq // EPOCH], 1)
                    seq += 1
                else:
                    ins.then_inc(dsem[dkey], 16)
            if eng_name == "sp":
                for k, n in P.dcnt.items():
                    e.wait_ge(dsem[k], 16 * n)
                for other in ("pe", "act", "dve"):
                    c = P.cnt[other]
                    if c:
                        e.wait_ge(esem[other][(c - 1) // EPOCH], (c - 1) % EPOCH + 1)

        with nc.Block() as block:
            @block.tensor
            def _(e):
                emit("pe", e)

            @block.scalar
            def _(e):
                emit("act", e)

            @block.vector
            def _(e):
                emit("dve", e)

            @block.gpsimd
            def _(e):
                emit("pool", e)

            @block.sync
            def _(e):
                emit("sp", e)
    return nc


def _fm(v, K):
    return np.ascontiguousarray(np.asarray(v, np.float32).reshape(K, 128).T)


def _units(W, KD):
    rows, ncols = W.shape
    assert rows == KD * 128
    return np.ascontiguousarray(W.reshape(KD, 128, ncols // 128, 128).transpose(2, 1, 0, 3)).reshape(ncols // 128, 128, KD * 128)


def prepare(cfg, inp):
    D, KD, KF, KH, depth = cfg.D, cfg.KD, cfg.KF, cfg.KH, cfg.depth
    f32 = np.float32
    units = []
    pps = np.zeros((depth, 128, cfg.ppw), f32)
    for i in range(depth):
        j = i // 2
        pr = pps[i]
        pr[:, cfg.o_mpre:cfg.o_mpre + KD] = _fm(inp["norm_mix_pre"][i], KD)
        pr[:, cfg.o_mpost:cfg.o_mpost + KD] = _fm(inp["norm_mix_post"][i], KD)
        pr[:, cfg.o_fpre:cfg.o_fpre + KD] = _fm(inp["norm_ffn_pre"][i], KD)
        pr[:, cfg.o_fpost:cfg.o_fpost + KD] = _fm(inp["norm_ffn_post"][i], KD)
        fw = np.asarray(inp["f_w_dw"][i], f32)
        pr[:, cfg.o_fw:cfg.o_fw + KF * CF] = fw.reshape(CF, KF, 128).transpose(2, 1, 0).reshape(128, KF * CF)
        pr[:, cfg.o_fb:cfg.o_fb + KF] = _fm(inp["f_b_dw"][i], KF)
        om = cfg.o_mix
        if i % 2 == 0:
            w_in = np.asarray(inp["a_w_in"][j], f32)
            u_in = _units(w_in, KD)
            units.append(u_in[KD:2 * KD])
            units.append(u_in[0:KD])
            units.append(_units(np.asarray(inp["a_w_out"][j], f32), KD))
            pr[:, om:om + 2 * KD] = _fm(inp["a_b_in"][j], 2 * KD)
            pr[:, om + 2 * KD:om + 3 * KD] = _fm(inp["a_ln_g"][j], KD)
            pr[:, om + 3 * KD:om + 4 * KD] = _fm(inp["a_ln_b"][j], KD)
        else:
            w_in = np.asarray(inp["b_w_in"][j], f32)
            u_in = _units(w_in, KD)
            inter = np.empty((2 * KD,) + u_in.shape[1:], f32)
            inter[0::2] = u_in[0:KD]
            inter[1::2] = u_in[KD:2 * KD]
            units.append(inter)
            units.append(_units(np.asarray(inp["b_w_out"][j], f32), KD))
            pr[:, om:om + 2 * KD] = _fm(inp["b_b_in"][j], 2 * KD)
            wd = np.asarray(inp["b_w_dw"][j], f32)
            pr[:, om + 2 * KD:om + 2 * KD + CB * KD] = wd.reshape(CB, KD, 128).transpose(2, 1, 0).reshape(128, KD * CB)
            o_b = om + 2 * KD + CB * KD
            pr[:, o_b:o_b + KD] = _fm(inp["b_b_dw"][j], KD)
            pr[:, o_b + KD:o_b + 2 * KD] = _fm(inp["b_ln_g"][j], KD)
            pr[:, o_b + 2 * KD:o_b + 3 * KD] = _fm(inp["b_ln_b"][j], KD)
        u_up = _units(np.asarray(inp["f_w_up"][i], f32), KD)
        inter = np.empty_like(u_up)
        inter[0::2] = u_up[0:KH]
        inter[1::2] = u_up[KH:2 * KH]
        units.append(inter)
        wdn = np.asarray(inp["f_w_down"][i], f32)
        d0 = _units(wdn[0:D], KD)
        d1 = _units(wdn[D:2 * D], KD)
        inter = np.empty((2 * KD,) + d0.shape[1:], f32)
        inter[0::2] = d0
        inter[1::2] = d1
        units.append(inter)
    ws = np.concatenate(units, axis=0)
    assert ws.shape[0] == cfg.NU, (ws.shape, cfg.NU)
    ws = ws.reshape(cfg.NU * 128, KD * 128)
    wst = np.ascontiguousarray(np.asarray(inp["a_w_s"], f32).transpose(0, 3, 1, 2)).reshape(cfg.NA * 128, GA * 128)
    bs = np.ascontiguousarray(np.asarray(inp["a_b_s"], f32)).reshape(cfg.NA, GA * 128)
    shared = dict(ws=ws, pp=pps.reshape(depth * 128, cfg.ppw), wst=wst, bs=bs,
                  ident=np.eye(128, dtype=f32))
    x_prompt = np.asarray(inp["x_prompt"], f32)
    x_sample = np.asarray(inp["x_sample"], f32)
    sc = np.asarray(inp["state_conv"], f32)
    sf = np.asarray(inp["state_ffn"], f32)
    per_seq = x_prompt.shape[1] // SHARD
    in_maps = []
    for c in range(NCORES):
        b, q = divmod(c, per_seq)
        s0 = q * SHARD
        xpc = np.zeros((HALO + SHARD, D), f32)
        lo = max(0, s0 - HALO)
        xpc[HALO - (s0 - lo):] = x_prompt[b, lo:s0 + SHARD]
        m = dict(shared)
        m["xp"] = xpc
        m["xs"] = np.ascontiguousarray(x_sample[c])
        m["valid"] = np.full((128, 1), 0.0 if q == 0 else 1.0, f32)
        m["sconv"] = np.ascontiguousarray(sc[:, c]).reshape(cfg.NB * (CB - 1), D)
        m["sffn"] = np.ascontiguousarray(sf[:, c]).reshape(depth * (CF - 1), 4 * D)
        in_maps.append(m)
    return in_maps


def assemble(cfg, res, batch, seq):
    D, depth = cfg.D, cfg.depth
    per_seq = seq // SHARD
    y_prompt = np.empty((batch, seq, D), np.float32)
    for c in range(NCORES):
        b, q = divmod(c, per_seq)
        y_prompt[b, q * SHARD:(q + 1) * SHARD] = res[c]["yp"]
    y_sample = np.stack([res[c]["ys"] for c in range(NCORES)])
    last = [b * per_seq + per_seq - 1 for b in range(batch)]
    ncp = np.stack([res[c]["ncp"].reshape(cfg.NB, CB - 1, D) for c in last], axis=1)
    nfp = np.stack([res[c]["nfp"].reshape(depth, CF - 1, 4 * D) for c in last], axis=1)
    ncs = np.stack([res[c]["ncs"].reshape(cfg.NB, CB - 1, D) for c in range(NCORES)], axis=1)
    nfs = np.stack([res[c]["nfs"].reshape(depth, CF - 1, 4 * D) for c in range(NCORES)], axis=1)
    nvs = np.stack([res[c]["nvs"].reshape(cfg.NA, NS, D) for c in range(NCORES)], axis=1)
    return (y_prompt, y_sample, ncp, nfp, ncs, nfs, nvs)


_CACHE = {}


def run(cfg, inputs):
    key = (cfg.D, cfg.depth)
    if key not in _CACHE:
        _CACHE[key] = build(cfg)
    nc = _CACHE[key]
    in_maps = prepare(cfg, inputs)
    res = run_bass_kernel_spmd(nc, in_maps, core_ids=list(range(NCORES)))
    xpr = np.asarray(inputs["x_prompt"])
    return assemble(cfg, res.results, xpr.shape[0], xpr.shape[1])


def kernel(**inputs):
    return run(Cfg(4096, 4), inputs)
```

```python
import numpy as np
from contextlib import ExitStack
import concourse.bass as bass
import concourse.mybir as mybir
from concourse.bass_utils import run_bass_kernel_spmd

F32 = mybir.dt.float32
BF16 = mybir.dt.bfloat16
AF = mybir.ActivationFunctionType
ALU = mybir.AluOpType
EPS = 1e-6
NCORES = 8
HALO = 256
SHARD = 1024
NS = 64
CB = 31
CF = 3
GA = 8
TMAX = 384
NW = 4
EPOCH = 8192
ENG = ("pe", "act", "dve", "pool", "sp")


class Cfg:
    def __init__(self, D=4096, depth=4):
        self.D = D
        self.KD = D // 128
        self.KF = 4 * self.KD
        self.KH = 2 * self.KD
        self.depth = depth
        self.NA = (depth + 1) // 2
        self.NB = depth // 2
        self.cpg = max(1, self.KD // GA)
        KD, KF = self.KD, self.KF
        o = 0
        self.o_mpre = o; o += KD
        self.o_mpost = o; o += KD
        self.o_fpre = o; o += KD
        self.o_fpost = o; o += KD
        self.o_fw = o; o += KF * CF
        self.o_fb = o; o += KF
        self.o_mix = o
        self.ppw = o + max(4 * KD, 2 * KD + CB * KD + 3 * KD)
        self.units_per_layer = 3 * KD + KF + 2 * KD
        self.NU = depth * self.units_per_layer
        self.tiles = [
            dict(T=384, segs=[("p", 0, 384, 0, 0, HALO)]),
            dict(T=384, segs=[("p", 0, 384, 384, 128, 0)]),
            dict(T=384, segs=[("p", 0, 384, 768, 512, 0)]),
            dict(T=192, segs=[("p", 0, 128, 1152, 896, 0), ("s", 128, 64, 0, 0, 0)]),
        ]


class Prog:
    def __init__(self):
        self.ops = {e: [] for e in ENG}
        self.cnt = {e: 0 for e in ENG}
        self.seen = {e: {} for e in ENG}
        self.lastw = {}
        self.readers = {}
        self.dcnt = {}

    def _deps(self, eng, reads, writes):
        raw, other = set(), set()
        for r in reads:
            t = self.lastw.get(r)
            if t is not None:
                raw.add(t)
            if r == "ptb" or (isinstance(r, tuple) and r[0] in ("pm", "pst", "ptr")):
                rd = self.readers.get(r)
                if rd:
                    other.update(rd.values())
        for w in writes:
            t = self.lastw.get(w)
            if t is not None:
                other.add(t)
            rd = self.readers.get(w)
            if rd:
                other.update(rd.values())
        waits = {}
        seen = self.seen[eng]
        for t in raw | other:
            if t[0] == "e":
                src, seq = t[1], t[2]
                if src == eng and (eng == "pe" or t not in raw):
                    continue
                key = src
            else:
                key, seq = ("d", t[1]), t[2]
            if seen.get(key, -1) >= seq:
                continue
            if waits.get(key, -1) < seq:
                waits[key] = seq
        for k, v in waits.items():
            seen[k] = v
        return list(waits.items())

    def _register(self, tok, reads, writes):
        src = tok[1]
        for r in reads:
            self.readers.setdefault(r, {})[src] = tok
        for w in writes:
            self.lastw[w] = tok
            self.readers[w] = {}

    stop = False
    stop_at = None

    def mark(self, name):
        if self.stop_at is not None and name == self.stop_at:
            self.stop = True

    def op(self, eng, fn, reads=(), writes=()):
        if self.stop:
            return
        waits = self._deps(eng, reads, writes)
        seq = self.cnt[eng]
        self.cnt[eng] += 1
        self.ops[eng].append((waits, fn, None))
        self._register(("e", eng, seq), reads, writes)

    def dma(self, eng, fn, key, reads=(), writes=()):
        if self.stop:
            return
        waits = self._deps(eng, reads, writes)
        n = self.dcnt.get(key, 0) + 1
        self.dcnt[key] = n
        self.ops[eng].append((waits, fn, key))
        self._register(("d", key, n), reads, writes)


def build(cfg):
    D, KD, KF, KH = cfg.D, cfg.KD, cfg.KF, cfg.KH
    depth, NA, NB, cpg = cfg.depth, cfg.NA, cfg.NB, cfg.cpg
    HB, HF = CB - 1, CF - 1
    nc = bass.Bass("TRN2", target_bir_lowering=False)

    def din(name, shape):
        return nc.dram_tensor(name, shape, F32, kind="ExternalInput").ap()

    def dout(name, shape):
        return nc.dram_tensor(name, shape, F32, kind="ExternalOutput").ap()

    xp = din("xp", [HALO + SHARD, D])
    xs = din("xs", [NS, D])
    valid_d = din("valid", [128, 1])
    ident_d = din("ident", [128, 128])
    sconv = din("sconv", [NB * HB, D])
    sffn = din("sffn", [depth * HF, 4 * D])
    ws = din("ws", [cfg.NU * 128, KD * 128])
    pp_d = din("pp", [depth * 128, cfg.ppw])
    wst_d = din("wst", [NA * 128, GA * 128])
    bs_d = din("bs", [NA, GA * 128])
    upc = max(1, (240 << 20) // (128 * KD * 128 * 2))
    wcs = [nc.dram_tensor(f"wcache{i}", [min(upc, cfg.NU - i * upc) * 128, KD * 128], BF16, kind="Internal").ap()
           for i in range((cfg.NU + upc - 1) // upc)]
    yp = dout("yp", [SHARD, D])
    ys = dout("ys", [NS, D])
    ncp = dout("ncp", [NB * HB, D])
    nfp = dout("nfp", [depth * HF, 4 * D])
    ncs = dout("ncs", [NB * HB, D])
    nfs = dout("nfs", [depth * HF, 4 * D])
    nvs = dout("nvs", [NA * NS, D])

    P = Prog()
    P.stop_at = (getattr(cfg, "dbg", None) or {}).get("stop_at")
    EXTW = 416
    es = ExitStack()
    with es:
        def sb(name, shape, dt):
            return es.enter_context(nc.sbuf_tensor("sb_" + name, shape, dt))

        def psum(name, shape, dt):
            return es.enter_context(nc.psum_tensor("ps_" + name, shape, dt))

        xT = sb("xT", [128, KD, TMAX], F32)
        hT = sb("hT", [128, KD, TMAX], BF16)
        big_f = sb("big", [128, KD, TMAX], F32)
        big_b = sb_b = big_f.bitcast(BF16)
        wsl = sb("wsl", [128, NW, KD * 128], BF16)
        pp = sb("pp", [128, cfg.ppw], F32)
        Hg_p = sb("Hg_p", [128, NB * KD, HB], F32)
        Hg_s = sb("Hg_s", [128, KD, HB], F32)
        Hu_p = sb("Hu_p", [128, depth * KF, HF], F32)
        Hu_s = sb("Hu_s", [128, KF, HF], F32)
        ext = sb("ext", [128, 4, EXTW], F32)
        scr = sb("scr", [128, 9, TMAX], F32)
        scr_b = scr.bitcast(BF16)
        ident_f = sb("ident_f", [128, 128], F32)
        ident_b = sb("ident_b", [128, 128], BF16)
        ones_b = sb("ones_b", [128, 128], BF16)
        wst_b = sb("wst_b", [128, GA * 128], BF16)
        bs_b = sb("bs_b", [1, 2 * GA * 128], BF16)
        so_f = sb("so_f", [128, 2, 512], F32)
        so_flat = so_f[:].rearrange("p a b -> p (a b)")
        ext_flat = ext[:].rearrange("p a b -> p (a b)")
        valid = sb("valid_sb", [128, 1], F32)
        epsc = sb("epsc", [128, 1], F32)

        pm = psum("pm", [128, 3, 512], F32)
        pst = psum("pst", [128, 2, 512], F32)
        ptr = psum("ptr", [128, 2, 512], F32)
        ptb = psum("ptb", [128, 1, 1024], BF16)

        def bigb(j, c0, c1):
            return big_b[:, j // 2, (j % 2) * TMAX + c0:(j % 2) * TMAX + c1]

        hT_flat = hT[:].rearrange("p k t -> p (k t)")
        big_flat = big_f[:].rearrange("p k t -> p (k t)")

        def vtok(s, r0, r1, c0, c1):
            return hT_flat[r0:r1, s * D + c0:s * D + c1]

        def vtok_keys(s):
            lo = (s * D) // TMAX
            hi = ((s + 1) * D + TMAX - 1) // TMAX
            return [("h", k) for k in range(lo, min(hi, KD))]

        def stage(s, r0, r1, c0, c1):
            return big_flat[r0:r1, s * D + c0:s * D + c1]

        def stage_keys(s):
            lo = (s * D * 4) // (TMAX * 2)
            hi = ((s + 1) * D * 4 + TMAX * 2 - 1) // (TMAX * 2)
            return [("big", j) for j in range(lo, min(hi, 2 * KD))]

        NSTAGE = 2
        rot = {}

        def nxt(name, n):
            v = rot.get(name, 0)
            rot[name] = v + 1
            return v % n

        P.dma("sp", lambda e: e.dma_start(out=ident_f[:], in_=ident_d), "c_ident", writes=["ident_f"])
        P.dma("sp", lambda e: e.dma_start(out=valid[:], in_=valid_d), "c_valid", writes=["valid"])
        P.op("act", lambda e: e.copy(out=ident_b[:], in_=ident_f[:]), reads=["ident_f"], writes=["ident_b"])
        P.op("dve", lambda e: e.memset(ones_b[:], 1.0), writes=["ones_b"])
        P.op("dve", lambda e: e.memset(epsc[:], EPS), writes=["epsc"])
        P.op("dve", lambda e: e.memset(Hg_p[:], 0.0), writes=[("Hg_p", j, c) for j in range(NB) for c in range(KD)])
        P.op("dve", lambda e: e.memset(Hu_p[:], 0.0), writes=[("Hu_p", i, c) for i in range(depth) for c in range(KF)])

        wstate = dict(next_dma=0, next_use=0)

        def w_dma(u):
            slot = u % NW
            pas, uu = divmod(u, cfg.NU)
            early = (uu % 3 == 0)
            rows = slice(uu * 128, (uu + 1) * 128)
            wc = wcs[uu // upc]
            crow = slice((uu % upc) * 128, (uu % upc + 1) * 128)
            from_cache = (pas >= 2) or (pas == 1 and early)
            if from_cache:
                P.dma("pool", lambda e: e.dma_start(out=wsl[:, slot, :], in_=wc[crow, :]),
                      ("w", slot), reads=[("wc", uu)], writes=[("w", slot)])
                return
            P.dma("pool", lambda e: e.dma_start(out=wsl[:, slot, :], in_=ws[rows, :]),
                  ("w", slot), writes=[("w", slot)])
            if len(cfg.tiles) > 2 and ((pas == 0 and early) or pas == 1):
                P.dma("sp", lambda e: e.dma_start(out=wc[crow, :], in_=wsl[:, slot, :]),
                      ("wb", slot), reads=[("w", slot)], writes=[("wc", uu)])

        def mm_unit(bank, T, rhs_fn, rhs_keys, start=True, stop=True):
            u = wstate["next_use"]
            wstate["next_use"] = u + 1
            while wstate["next_dma"] <= min(u + NW - 1, total_units - 1):
                w_dma(wstate["next_dma"])
                wstate["next_dma"] += 1
            slot = u % NW

            def fn(e):
                ins = None
                for k in range(KD):
                    ins = e.matmul(pm[:, bank, 0:T], wsl[:, slot, k * 128:(k + 1) * 128], rhs_fn(k),
                                   start=(start and k == 0), stop=(stop and k == KD - 1))
                return ins
            P.op("pe", fn, reads=[("w", slot)] + rhs_keys, writes=[("pm", bank)])

        total_units = cfg.NU * len(cfg.tiles)

        def pcol(off, n=1):
            return pp[:, off:off + n]

        def load_pp(layer):
            P.dma("sp", lambda e: e.dma_start(out=pp[:], in_=pp_d[layer * 128:(layer + 1) * 128, :]),
                  "pp", writes=["pp"])

        def stat_finish_rms(T, dst):
            P.op("act", lambda e: e.activation(out=scr[:, dst, 0:T], in_=pst[:, 0, 0:T], func=AF.Sqrt, bias=epsc[:, 0:1], scale=1.0 / D),
                 reads=[("pst", 0), "epsc"], writes=[("scr", dst)])
            P.op("dve", lambda e: e.reciprocal(out=scr[:, dst, 0:T], in_=scr[:, dst, 0:T]),
                 reads=[("scr", dst)], writes=[("scr", dst)])

        def rms_in(T, goff):
            for k in range(KD):
                s = 6 + nxt("sq", 2)
                P.op("act", lambda e, k=k, s=s: e.activation(out=scr_b[:, s, 0:T], in_=xT[:, k, 0:T], func=AF.Square),
                     reads=[("x", k)], writes=[("scr", s)])
                P.op("pe", lambda e, k=k, s=s: e.matmul(pst[:, 0, 0:T], ones_b[:, :], scr_b[:, s, 0:T],
                                                        start=(k == 0), stop=(k == KD - 1)),
                     reads=[("scr", s), "ones_b"], writes=[("pst", 0)])
            stat_finish_rms(T, 8)
            for k in range(KD):
                P.op("dve", lambda e, k=k: e.scalar_tensor_tensor(
                    out=hT[:, k, 0:T], in0=xT[:, k, 0:T], scalar=pcol(goff + k), in1=scr[:, 8, 0:T],
                    op0=ALU.mult, op1=ALU.mult),
                    reads=[("x", k), ("scr", 8), "pp"], writes=[("h", k)])

        pend = []

        def flush_pend(keep=0):
            while len(pend) > keep:
                pend.pop(0)()

        def post_evac(bank, T, j):
            s = 6 + nxt("sq", 2)
            P.op("act", lambda e: e.activation(out=scr_b[:, s, 0:T], in_=pm[:, bank, 0:T], func=AF.Square),
                 reads=[("pm", bank)], writes=[("scr", s)])
            P.op("dve", lambda e: e.tensor_copy(out=hT[:, j, 0:T], in_=pm[:, bank, 0:T]),
                 reads=[("pm", bank)], writes=[("h", j)])
            pend.append(lambda: P.op("pe", lambda e: e.matmul(pst[:, 0, 0:T], ones_b[:, :], scr_b[:, s, 0:T],
                                                              start=(j == 0), stop=(j == KD - 1)),
                                     reads=[("scr", s), "ones_b"], writes=[("pst", 0)]))

        def post_finish(T, goff):
            flush_pend()
            stat_finish_rms(T, 8)
            def op2(j, s):
                P.op("dve", lambda e: e.tensor_tensor(
                    out=xT[:, j, 0:T], in0=xT[:, j, 0:T], in1=scr[:, s, 0:T], op=ALU.add),
                    reads=[("x", j), ("scr", s)], writes=[("x", j)])
            prev = None
            for j in range(KD):
                s = nxt("tmpx", 2)
                P.op("dve", lambda e, j=j, s=s: e.scalar_tensor_tensor(
                    out=scr[:, s, 0:T], in0=hT[:, j, 0:T], scalar=pcol(goff + j), in1=scr[:, 8, 0:T],
                    op0=ALU.mult, op1=ALU.mult),
                    reads=[("h", j), ("scr", 8), "pp"], writes=[("scr", s)])
                if prev is not None:
                    op2(*prev)
                prev = (j, s)
            op2(*prev)

        def ln_stats_finish(T, sc):
            P.op("dve", lambda e: e.tensor_scalar(out=scr[:, 8, 0:T], in0=pst[:, 0, 0:T], scalar1=sc,
                                                  scalar2=None, op0=ALU.mult),
                 reads=[("pst", 0)], writes=[("scr", 8)])
            P.op("dve", lambda e: e.tensor_tensor(out=scr[:, 6, 0:T], in0=scr[:, 8, 0:T], in1=scr[:, 8, 0:T],
                                                  op=ALU.mult),
                 reads=[("scr", 8)], writes=[("scr", 6)])
            P.op("dve", lambda e: e.scalar_tensor_tensor(out=scr[:, 7, 0:T], in0=pst[:, 1, 0:T], scalar=sc,
                                                         in1=scr[:, 6, 0:T], op0=ALU.mult, op1=ALU.subtract),
                 reads=[("pst", 1), ("scr", 6)], writes=[("scr", 7)])
            P.op("act", lambda e: e.activation(out=scr[:, 7, 0:T], in_=scr[:, 7, 0:T], func=AF.Sqrt, bias=epsc[:, 0:1]),
                 reads=[("scr", 7), "epsc"], writes=[("scr", 7)])
            P.op("dve", lambda e: e.reciprocal(out=scr[:, 7, 0:T], in_=scr[:, 7, 0:T]),
                 reads=[("scr", 7)], writes=[("scr", 7)])
            P.op("dve", lambda e: e.scalar_tensor_tensor(out=scr[:, 8, 0:T], in0=scr[:, 8, 0:T], scalar=-1.0,
                                                         in1=scr[:, 7, 0:T], op0=ALU.mult, op1=ALU.mult),
                 reads=[("scr", 8), ("scr", 7)], writes=[("scr", 8)])

        def out_rows_T(src_fn, src_keys, nchunks, nrows, dst, dst_row0):
            for r0 in range(0, nrows, 4):
                nr = min(4, nrows - r0)
                b = nxt("ptr", 2)
                s = nxt("so", 2)

                def ft(e, nr=nr, r0=r0, b=b):
                    ins = None
                    for rr in range(nr):
                        ins = e.transpose(ptr[0:nchunks, b, rr * 128:(rr + 1) * 128], src_fn(r0 + rr),
                                          ident_f[:, :])
                    return ins
                P.op("pe", ft, reads=src_keys + ["ident_f"], writes=[("ptr", b)])
                P.op("act", lambda e, nr=nr, b=b, s=s: e.copy(out=so_f[0:nchunks, s, 0:nr * 128], in_=ptr[0:nchunks, b, 0:nr * 128]),
                     reads=[("ptr", b)], writes=[("so", s)])
                P.dma("sp", lambda e, nr=nr, r0=r0, s=s: e.dma_start(
                    out=dst[dst_row0 + r0:dst_row0 + r0 + nr, :].rearrange("r (c p) -> c r p", p=128),
                    in_=so_f[0:nchunks, s, 0:nr * 128].rearrange("c (r p) -> c r p", p=128)),
                    ("so", s), reads=[("so", s)])

        def in_rows_T(src, src_row0, nrows, nchunks, dst_fn, dst_keys):
            for c0 in range(0, nchunks, 4):
                b = nxt("ptr", 2)
                s = nxt("so", 2)
                P.dma("sp", lambda e, s=s, c0=c0: e.dma_start(out=so_f[0:nrows, s, :],
                                                   in_=src[src_row0:src_row0 + nrows, c0 * 128:(c0 + 4) * 128]),
                      ("so", s), writes=[("so", s)])

                def ft(e, s=s, b=b):
                    ins = None
                    for q in range(4):
                        ins = e.transpose(ptr[:, b, q * 128:q * 128 + nrows], so_f[0:nrows, s, q * 128:(q + 1) * 128],
                                          ident_f[0:nrows, 0:nrows])
                    return ins
                P.op("pe", ft, reads=[("so", s), "ident_f"], writes=[("ptr", b)])
                P.op("act", lambda e, c0=c0, b=b: e.copy(
                    out=dst_fn(c0, 4),
                    in_=ptr[:, b, :].rearrange("p (q r) -> p q r", r=128)[:, :, 0:nrows]),
                    reads=[("ptr", b)], writes=[(dst_keys, c0 + q) for q in range(4)])

        def tile_load(tile):
            for (kind, col0, ln, src0, own0, oskip) in tile["segs"]:
                src = xp if kind == "p" else xs
                for r0 in range(0, ln, 128):
                    rows = min(128, ln - r0)
                    s = nxt("stage", NSTAGE)
                    P.dma("sp", lambda e, s=s, rows=rows, r0=r0, src=src, src0=src0: e.dma_start(
                        out=stage(s, 0, rows, 0, D), in_=src[src0 + r0:src0 + r0 + rows, :]),
                        ("stage", s), writes=stage_keys(s))
                    for k0 in range(0, KD, 4):
                        b = nxt("ptr", 2)

                        def ft(e, s=s, rows=rows, k0=k0, b=b):
                            ins = None
                            for q in range(4):
                                ins = e.transpose(ptr[:, b, q * 128:q * 128 + rows],
                                                  stage(s, 0, rows, (k0 + q) * 128, (k0 + q + 1) * 128),
                                                  ident_f[0:rows, 0:rows])
                            return ins
                        P.op("pe", ft, reads=stage_keys(s) + ["ident_f"], writes=[("ptr", b)])
                        P.op("act", lambda e, rows=rows, k0=k0, b=b, c=col0 + r0: e.copy(
                            out=xT[:, k0:k0 + 4, c:c + rows],
                            in_=ptr[:, b, :].rearrange("p (q r) -> p q r", r=128)[:, :, 0:rows]),
                            reads=[("ptr", b)], writes=[("x", k0 + q) for q in range(4)])

        def tile_store(tile):
            for (kind, col0, ln, src0, own0, oskip) in tile["segs"]:
                if own0 is None:
                    continue
                dst = yp if kind == "p" else ys
                for r0 in range(oskip, ln, 128):
                    rows = min(128, ln - r0)
                    s = nxt("stage", NSTAGE)
                    for k0 in range(0, KD, 4):
                        b = nxt("ptr", 2)

                        def ft(e, rows=rows, k0=k0, b=b, c=col0 + r0):
                            ins = None
                            for q in range(4):
                                ins = e.transpose(ptr[0:rows, b, q * 128:(q + 1) * 128],
                                                  xT[:, k0 + q, c:c + rows], ident_f[:, :])
                            return ins
                        P.op("pe", ft, reads=[("x", k0 + q) for q in range(4)] + ["ident_f"], writes=[("ptr", b)])
                        P.op("act", lambda e, s=s, rows=rows, k0=k0, b=b: e.copy(
                            out=stage(s, 0, rows, k0 * 128, (k0 + 4) * 128), in_=ptr[0:rows, b, :]),
                            reads=[("ptr", b)], writes=stage_keys(s))
                    P.dma("sp", lambda e, s=s, rows=rows, dst=dst, o=own0 + r0 - oskip: e.dma_start(
                        out=dst[o:o + rows, :], in_=stage(s, 0, rows, 0, D)),
                        ("stage", s), reads=stage_keys(s))

        def ext_layout(tile, H):
            segs = []
            o = 0
            for (kind, col0, ln, src0, own0, oskip) in tile["segs"]:
                segs.append((kind, col0, ln, o))
                o += H + ln
            return segs, o

        def ffn(ti, tile, layer):
            T = tile["T"]
            last_p = (ti == len(cfg.tiles) - 1)
            has_s = any(sg[0] == "s" for sg in tile["segs"])
            P.mark("ffn.start")
            rms_in(T, cfg.o_fpre)
            P.mark("ffn.rms")
            segs, W = ext_layout(tile, HF)
            if has_s:
                in_rows_T(sffn, layer * HF, HF, KF, lambda c0, n: Hu_s[:, c0:c0 + n, :], "Hu_s")
            P.mark("ffn.inrows")
            for c in range(KH):
                par = c % 2
                if c == 1:
                    P.mark("ffn.pair0")
                for half in range(2):
                    cc = c + half * KH
                    bank = nxt("pm", 3)
                    mm_unit(bank, T, lambda k: hT[:, k, 0:T], [("h", k) for k in range(KD)])
                    eb = 2 * par + half
                    cv = 2 * half + par
                    for (kind, col0, ln, eo) in segs:
                        if kind == "p":
                            Hsrc = Hu_p[:, layer * KF + cc, :]
                            hk = ("Hu_p", layer, cc)
                        else:
                            Hsrc = Hu_s[:, cc, :]
                            hk = ("Hu_s", cc)
                        P.op("dve", lambda e, eb=eb, eo=eo, Hsrc=Hsrc: e.tensor_copy(out=ext[:, eb, eo:eo + HF], in_=Hsrc),
                             reads=[hk], writes=[("ext", eb)])
                        P.op("act", lambda e, eb=eb, eo=eo, ln=ln, col0=col0, bank=bank: e.copy(
                            out=ext[:, eb, eo + HF:eo + HF + ln], in_=pm[:, bank, col0:col0 + ln]),
                            reads=[("pm", bank)], writes=[("ext", eb)])
                        if kind == "p" and ti == 0:
                            P.op("dve", lambda e, eb=eb, eo=eo: e.tensor_scalar(
                                out=ext[:, eb, eo + HALO:eo + HALO + HF], in0=ext[:, eb, eo + HALO:eo + HALO + HF],
                                scalar1=valid[:, 0:1], scalar2=None, op0=ALU.mult),
                                reads=[("ext", eb), "valid"], writes=[("ext", eb)])
                        P.op("dve", lambda e, eb=eb, eo=eo, ln=ln, Hsrc=Hsrc: e.tensor_copy(
                            out=Hsrc, in_=ext[:, eb, eo + ln:eo + ln + HF]), reads=[("ext", eb)], writes=[hk])
                    wo = cfg.o_fw + cc * CF
                    P.op("act", lambda e, eb=eb, cv=cv, wo=wo, cc=cc: e.activation(
                        out=scr[:, cv, 0:W - HF], in_=ext[:, eb, HF:W], func=AF.Identity,
                        scale=pcol(wo + 2), bias=pcol(cfg.o_fb + cc)),
                        reads=[("ext", eb), "pp"], writes=[("scr", cv)])
                    for k in (1, 0):
                        P.op("dve", lambda e, eb=eb, cv=cv, wo=wo, k=k: e.scalar_tensor_tensor(
                            out=scr[:, cv, 0:W - HF], in0=ext[:, eb, k:k + W - HF], scalar=pcol(wo + k),
                            in1=scr[:, cv, 0:W - HF], op0=ALU.mult, op1=ALU.add),
                            reads=[("ext", eb), ("scr", cv), "pp"], writes=[("scr", cv)])
                cg, cvv = par, 2 + par
                P.op("act", lambda e, cg=cg: e.activation(out=scr[:, cg, 0:W - HF], in_=scr[:, cg, 0:W - HF], func=AF.Silu),
                     reads=[("scr", cg)], writes=[("scr", cg)])
                for (kind, col0, ln, eo) in segs:
                    P.op("dve", lambda e, cg=cg, cvv=cvv, c=c, col0=col0, ln=ln, eo=eo: e.tensor_tensor(
                        out=bigb(c, col0, col0 + ln), in0=scr[:, cg, eo:eo + ln], in1=scr[:, cvv, eo:eo + ln],
                        op=ALU.mult), reads=[("scr", cg), ("scr", cvv)], writes=[("big", c)])
            P.mark("ffn.up")
            if has_s:
                out_rows_T(lambda r: Hu_s[:, :, r], [("Hu_s", c) for c in range(KF)], KF, HF, nfs, layer * HF)
            if last_p:
                out_rows_T(lambda r: Hu_p[:, layer * KF:(layer + 1) * KF, r], [("Hu_p", layer, c) for c in range(KF)], KF, HF, nfp, layer * HF)
            P.mark("ffn.outrows")
            for j in range(KD):
                bank = nxt("pm", 3)
                for half in range(2):
                    mm_unit(bank, T, lambda k, half=half: bigb(half * KD + k, 0, T),
                            [("big", half * KD + k) for k in range(KD)], start=(half == 0), stop=(half == 1))
                flush_pend()
                post_evac(bank, T, j)
            P.mark("ffn.down")
            post_finish(T, cfg.o_fpost)
            P.mark("ffn.end")

        def gmlp_blocks(tile):
            blocks = []
            for (kind, col0, ln, src0, own0, oskip) in tile["segs"]:
                for r0 in range(0, ln, 128):
                    blocks.append((col0 + r0, min(128, ln - r0), kind))
            return blocks

        def gmlp(ti, tile, layer):
            T = tile["T"]
            ja = layer // 2
            om = cfg.o_mix
            o_bin, o_lng, o_lnb = om, om + 2 * KD, om + 3 * KD
            SO = [("so", 0), ("so", 1)]
            EX = [("ext", 0), ("ext", 1), ("ext", 2)]
            P.dma("sp", lambda e: e.dma_start(out=so_flat[:, :], in_=wst_d[ja * 128:(ja + 1) * 128, :]), "wst",
                  writes=SO)
            P.op("act", lambda e: e.copy(out=wst_b[:], in_=so_flat[:, :]), reads=SO, writes=["wst_b"])
            wv = wst_b[:].rearrange("p (g i) -> p g i", i=128)
            P.op("dve", lambda e: e.memset(wv[64:128, :, 0:64], 0.0), reads=["wst_b"], writes=["wst_b"])
            P.dma("sp", lambda e: e.dma_start(out=ext_flat[0:1, 0:GA * 128], in_=bs_d[ja:ja + 1, :]), "bsd", writes=EX)
            P.op("act", lambda e: e.copy(out=bs_b[:, 0:GA * 128], in_=ext_flat[0:1, 0:GA * 128]), reads=EX, writes=["bs_b"])
            P.op("act", lambda e: e.copy(out=so_flat[0:1, :], in_=bs_b[:, 0:GA * 128]), reads=["bs_b"], writes=SO)
            P.op("dve", lambda e: e.tensor_tensor(out=bs_b[:, GA * 128:2 * GA * 128], in0=ext_flat[0:1, 0:GA * 128],
                                                  in1=so_flat[0:1, :], op=ALU.subtract),
                 reads=EX + SO + ["bs_b"], writes=["bs_b"])
            rms_in(T, cfg.o_mpre)
            hk = [("h", k) for k in range(KD)]
            for c in range(KD):
                bank = nxt("pm", 3)
                mm_unit(bank, T, lambda k: hT[:, k, 0:T], hk)
                flush_pend()
                P.op("act", lambda e, c=c, bank=bank: e.activation(
                    out=bigb(KD + c, 0, T), in_=pm[:, bank, 0:T], func=AF.Gelu, bias=pcol(o_bin + KD + c)),
                    reads=[("pm", bank), "pp"], writes=[("big", KD + c)])
                s = 4 + nxt("sqb", 2)
                P.op("act", lambda e, c=c, s=s: e.activation(out=scr_b[:, s, 0:T], in_=bigb(KD + c, 0, T), func=AF.Square),
                     reads=[("big", KD + c)], writes=[("scr", s)])

                def vstats(c=c, s=s):
                    P.op("pe", lambda e: e.matmul(pst[:, 0, 0:T], ones_b[:, :], bigb(KD + c, 0, T),
                                                  start=(c == 0), stop=(c == KD - 1)),
                         reads=[("big", KD + c), "ones_b"], writes=[("pst", 0)])
                    P.op("pe", lambda e: e.matmul(pst[:, 1, 0:T], ones_b[:, :], scr_b[:, s, 0:T],
                                                  start=(c == 0), stop=(c == KD - 1)),
                         reads=[("scr", s), "ones_b"], writes=[("pst", 1)])
                pend.append(vstats)
            flush_pend()
            ln_stats_finish(T, 1.0 / D)
            for c in range(KD):
                bank = nxt("pm", 3)
                mm_unit(bank, T, lambda k: hT[:, k, 0:T], hk)
                P.op("act", lambda e, c=c, bank=bank: e.activation(
                    out=bigb(c, 0, T), in_=pm[:, bank, 0:T], func=AF.Gelu, bias=pcol(o_bin + c)),
                    reads=[("pm", bank), "pp"], writes=[("big", c)])
                s = nxt("nrm", 2)
                P.op("dve", lambda e, c=c, s=s: e.tensor_tensor(out=scr[:, s, 0:T], in0=bigb(KD + c, 0, T),
                                                                in1=scr[:, 7, 0:T], op=ALU.mult),
                     reads=[("big", KD + c), ("scr", 7)], writes=[("scr", s)])
                P.op("dve", lambda e, s=s: e.tensor_tensor(out=scr[:, s, 0:T], in0=scr[:, s, 0:T], in1=scr[:, 8, 0:T],
                                                           op=ALU.add),
                     reads=[("scr", s), ("scr", 8)], writes=[("scr", s)])
                P.op("act", lambda e, c=c, s=s: e.activation(
                    out=bigb(KD + c, 0, T), in_=scr[:, s, 0:T], func=AF.Identity,
                    scale=pcol(o_lng + c), bias=pcol(o_lnb + c)),
                    reads=[("scr", s), "pp"], writes=[("big", KD + c)])
            for (bc, rows, kind) in gmlp_blocks(tile):
                vs = nxt("vtok", 3)
                vk = vtok_keys(vs)
                for c0 in range(0, KD, 8):
                    def ft(e, c0=c0, bc=bc, rows=rows):
                        ins = None
                        for q in range(8):
                            ins = e.transpose(ptb[0:rows, 0, q * 128:(q + 1) * 128],
                                              bigb(KD + c0 + q, bc, bc + rows), ident_b[:, :])
                        return ins
                    P.op("pe", ft, reads=[("big", KD + c0 + q) for q in range(8)] + ["ident_b"], writes=["ptb"])
                    P.op("act", lambda e, c0=c0, rows=rows, vs=vs: e.copy(
                        out=vtok(vs, 0, rows, c0 * 128, (c0 + 8) * 128), in_=ptb[0:rows, 0, :]),
                        reads=["ptb"], writes=vk)
                if kind == "s":
                    for g0 in range(0, D, 512):
                        s = nxt("so", 2)
                        P.op("act", lambda e, s=s, g0=g0, vs=vs, rows=rows: e.copy(
                            out=so_f[0:rows, s, :], in_=vtok(vs, 0, rows, g0, g0 + 512)),
                            reads=vk, writes=[("so", s)])
                        P.dma("sp", lambda e, s=s, g0=g0, rows=rows: e.dma_start(
                            out=nvs[ja * NS:ja * NS + rows, g0:g0 + 512], in_=so_f[0:rows, s, :]),
                            ("so", s), reads=[("so", s)])
                for c0 in range(0, KD, 4):
                    b = nxt("ptr", 2)

                    def fm(e, c0=c0, b=b, rows=rows, vs=vs):
                        ins = None
                        for q in range(4):
                            c = c0 + q
                            g = c // cpg
                            o = ptr[:, b, q * 128:q * 128 + rows]
                            e.matmul(o, vtok(vs, 0, rows, c * 128, (c + 1) * 128),
                                     wst_b[0:rows, g * 128:g * 128 + rows], start=True, stop=False)
                            e.matmul(o, ones_b[0:1, :], bs_b[0:1, g * 128:g * 128 + rows], start=False, stop=False)
                            ins = e.matmul(o, ones_b[0:1, :], bs_b[0:1, (GA + g) * 128:(GA + g) * 128 + rows],
                                           start=False, stop=True)
                        return ins
                    P.op("pe", fm, reads=vk + ["wst_b", "bs_b", "ones_b"], writes=[("ptr", b)])
                    P.op("dve", lambda e, c0=c0, b=b, rows=rows, bc=bc: e.tensor_tensor(
                        out=big_b[:, c0 // 2:c0 // 2 + 2, :].rearrange("p a (h t) -> p (a h) t", h=2)[:, :, bc:bc + rows],
                        in0=big_b[:, c0 // 2:c0 // 2 + 2, :].rearrange("p a (h t) -> p (a h) t", h=2)[:, :, bc:bc + rows],
                        in1=ptr[:, b, :].rearrange("p (q r) -> p q r", r=128)[:, :, 0:rows], op=ALU.mult),
                        reads=[("ptr", b)] + [("big", c0 + q) for q in range(4)],
                        writes=[("big", c0 + q) for q in range(4)])
            for j in range(KD):
                bank = nxt("pm", 3)
                mm_unit(bank, T, lambda k: bigb(k, 0, T), [("big", k) for k in range(KD)])
                flush_pend()
                post_evac(bank, T, j)
            post_finish(T, cfg.o_mpost)

        def conformer(ti, tile, layer):
            T = tile["T"]
            jb = layer // 2
            last_p = (ti == len(cfg.tiles) - 1)
            has_s = any(sg[0] == "s" for sg in tile["segs"])
            om = cfg.o_mix
            o_bin, o_w, o_b = om, om + 2 * KD, om + 2 * KD + CB * KD
            o_lng, o_lnb = o_b + KD, o_b + 2 * KD
            rms_in(T, cfg.o_mpre)
            segs, W = ext_layout(tile, HB)
            if has_s:
                in_rows_T(sconv, jb * HB, HB, KD, lambda c0, n: Hg_s[:, c0:c0 + n, :], "Hg_s")
            hk = [("h", k) for k in range(KD)]
            for c in range(KD):
                bA = nxt("pm", 3)
                mm_unit(bA, T, lambda k: hT[:, k, 0:T], hk)
                bB = nxt("pm", 3)
                mm_unit(bB, T, lambda k: hT[:, k, 0:T], hk)
                flush_pend(keep=2)
                sg = 4 + nxt("sig", 2)
                P.op("act", lambda e, c=c, bB=bB, sg=sg: e.activation(
                    out=scr[:, sg, 0:T], in_=pm[:, bB, 0:T], func=AF.Sigmoid, bias=pcol(o_bin + KD + c)),
                    reads=[("pm", bB), "pp"], writes=[("scr", sg)])
                eb = nxt("gext", 2)
                for (kind, col0, ln, eo) in segs:
                    if kind == "p":
                        Hsrc = Hg_p[:, jb * KD + c, :]
                        hkey = ("Hg_p", jb, c)
                    else:
                        Hsrc = Hg_s[:, c, :]
                        hkey = ("Hg_s", c)
                    P.op("act", lambda e, eb=eb, eo=eo, Hsrc=Hsrc: e.copy(out=ext[:, eb, eo:eo + HB], in_=Hsrc),
                         reads=[hkey], writes=[("ext", eb)])
                    P.op("dve", lambda e, eb=eb, eo=eo, ln=ln, col0=col0, bA=bA, sg=sg, c=c: e.scalar_tensor_tensor(
                        out=ext[:, eb, eo + HB:eo + HB + ln], in0=pm[:, bA, col0:col0 + ln], scalar=pcol(o_bin + c),
                        in1=scr[:, sg, col0:col0 + ln], op0=ALU.add, op1=ALU.mult),
                        reads=[("pm", bA), ("scr", sg), "pp"], writes=[("ext", eb)])
                    if kind == "p" and ti == 0:
                        P.op("dve", lambda e, eb=eb, eo=eo: e.tensor_scalar(
                            out=ext[:, eb, eo + HALO:eo + HALO + HB], in0=ext[:, eb, eo + HALO:eo + HALO + HB],
                            scalar1=valid[:, 0:1], scalar2=None, op0=ALU.mult),
                            reads=[("ext", eb), "valid"], writes=[("ext", eb)])
                    P.op("dve", lambda e, eb=eb, eo=eo, ln=ln, Hsrc=Hsrc: e.tensor_copy(
                        out=Hsrc, in_=ext[:, eb, eo + ln:eo + ln + HB]), reads=[("ext", eb)], writes=[hkey])
                i2 = nxt("acc", 2)
                ac, ab = 2 + i2, i2
                Wc = W - HB
                P.op("act", lambda e, eb=eb, ac=ac, c=c: e.activation(
                    out=scr[:, ac, 0:Wc], in_=ext[:, eb, HB:W], func=AF.Identity,
                    scale=pcol(o_w + c * CB + HB), bias=pcol(o_b + c)),
                    reads=[("ext", eb), "pp"], writes=[("scr", ac)])
                P.op("dve", lambda e, eb=eb, ab=ab, c=c: e.tensor_scalar(
                    out=scr[:, ab, 0:Wc], in0=ext[:, eb, 0:Wc], scalar1=pcol(o_w + c * CB), scalar2=None, op0=ALU.mult),
                    reads=[("ext", eb), "pp"], writes=[("scr", ab)])
                for k in range(1, HB):
                    tg = ac if k % 2 == 1 else ab
                    P.op("dve", lambda e, eb=eb, tg=tg, c=c, k=k: e.scalar_tensor_tensor(
                        out=scr[:, tg, 0:Wc], in0=ext[:, eb, k:k + Wc], scalar=pcol(o_w + c * CB + k),
                        in1=scr[:, tg, 0:Wc], op0=ALU.mult, op1=ALU.add),
                        reads=[("ext", eb), ("scr", tg), "pp"], writes=[("scr", tg)])
                for (kind, col0, ln, eo) in segs:
                    P.op("dve", lambda e, ac=ac, ab=ab, c=c, eo=eo, ln=ln, col0=col0: e.tensor_tensor(
                        out=big_f[:, c, col0:col0 + ln], in0=scr[:, ac, eo:eo + ln], in1=scr[:, ab, eo:eo + ln],
                        op=ALU.add),
                        reads=[("scr", ac), ("scr", ab)], writes=[("big", 2 * c), ("big", 2 * c + 1)])
                def cstats(c=c):
                    s = 6 + nxt("sq", 2)
                    P.op("act", lambda e: e.copy(out=scr_b[:, s, 0:T], in_=big_f[:, c, 0:T]),
                         reads=[("big", 2 * c), ("big", 2 * c + 1)], writes=[("scr", s)])
                    P.op("pe", lambda e: e.matmul(pst[:, 0, 0:T], ones_b[:, :], scr_b[:, s, 0:T],
                                                  start=(c == 0), stop=(c == KD - 1)),
                         reads=[("scr", s), "ones_b"], writes=[("pst", 0)])
                    s2 = 6 + nxt("sq", 2)
                    P.op("act", lambda e: e.activation(out=scr_b[:, s2, 0:T], in_=big_f[:, c, 0:T], func=AF.Square),
                         reads=[("big", 2 * c), ("big", 2 * c + 1)], writes=[("scr", s2)])
                    P.op("pe", lambda e: e.matmul(pst[:, 1, 0:T], ones_b[:, :], scr_b[:, s2, 0:T],
                                                  start=(c == 0), stop=(c == KD - 1)),
                         reads=[("scr", s2), "ones_b"], writes=[("pst", 1)])
                pend.append(cstats)
            flush_pend()
            if has_s:
                out_rows_T(lambda r: Hg_s[:, :, r], [("Hg_s", c) for c in range(KD)], KD, HB, ncs, jb * HB)
            if last_p:
                out_rows_T(lambda r: Hg_p[:, jb * KD:(jb + 1) * KD, r], [("Hg_p", jb, c) for c in range(KD)], KD, HB, ncp, jb * HB)
            ln_stats_finish(T, 1.0 / D)
            for c in range(KD):
                s = nxt("nrm", 2)
                P.op("dve", lambda e, c=c, s=s: e.tensor_tensor(out=scr[:, s, 0:T], in0=big_f[:, c, 0:T],
                                                                in1=scr[:, 7, 0:T], op=ALU.mult),
                     reads=[("big", 2 * c), ("big", 2 * c + 1), ("scr", 7)], writes=[("scr", s)])
                P.op("dve", lambda e, s=s: e.tensor_tensor(out=scr[:, s, 0:T], in0=scr[:, s, 0:T], in1=scr[:, 8, 0:T],
                                                           op=ALU.add),
                     reads=[("scr", s), ("scr", 8)], writes=[("scr", s)])
                P.op("act", lambda e, c=c, s=s: e.activation(
                    out=bigb(c, 0, T), in_=scr[:, s, 0:T], func=AF.Silu, scale=pcol(o_lng + c), bias=pcol(o_lnb + c)),
                    reads=[("scr", s), "pp"], writes=[("big", c)])
            for j in range(KD):
                bank = nxt("pm", 3)
                mm_unit(bank, T, lambda k: bigb(k, 0, T), [("big", k) for k in range(KD)])
                flush_pend()
                post_evac(bank, T, j)
            post_finish(T, cfg.o_mpost)

        dbg = getattr(cfg, "dbg", None) or {}
        for ti, tile in enumerate(cfg.tiles[:dbg.get("ntiles", 99)]):
            tile_load(tile)
            for layer in range(min(depth, dbg.get("nlayers", 99))):
                load_pp(layer)
                if dbg.get("mixer", True):
                    if layer % 2 == 0:
                        gmlp(ti, tile, layer)
                    else:
                        conformer(ti, tile, layer)
                if dbg.get("ffn", True):
                    ffn(ti, tile, layer)
            tile_store(tile)
        if not dbg:
            assert wstate["next_use"] == total_units, (wstate, total_units)

        n_ep = {e: (P.cnt[e] + EPOCH - 1) // EPOCH for e in ENG}
        esem = {e: [es.enter_context(nc.semaphore(f"s_{e}{i}")) for i in range(n_ep[e])] for e in ENG}
        dsem = {k: es.enter_context(nc.semaphore("d_" + "_".join(str(x) for x in (k if isinstance(k, tuple) else (k,)))))
                for k in P.dcnt}
        fin = es.enter_context(nc.semaphore("fin"))

        def emit(eng_name, e):
            seq = 0
            for (waits, fn, dkey) in P.ops[eng_name]:
                for (key, val) in waits:
                    if isinstance(key, tuple) and key[0] == "d":
                        e.wait_ge(dsem[key[1]], 16 * val)
                    else:
                        e.wait_ge(esem[key][val // EPOCH], val % EPOCH + 1)
                ins = fn(e)
                if dkey is None:
                    ins.then_inc(esem[eng_name][seq // EPOCH], 1)
                    seq += 1
                else:
                    ins.then_inc(dsem[dkey], 16)
            if eng_name == "sp":
                for k, n in P.dcnt.items():
                    e.wait_ge(dsem[k], 16 * n)
                for other in ("pe", "act", "dve"):
                    c = P.cnt[other]
                    if c:
                        e.wait_ge(esem[other][(c - 1) // EPOCH], (c - 1) % EPOCH + 1)

        with nc.Block() as block:
            @block.tensor
            def _(e):
                emit("pe", e)

            @block.scalar
            def _(e):
                emit("act", e)

            @block.vector
            def _(e):
                emit("dve", e)

            @block.gpsimd
            def _(e):
                emit("pool", e)

            @block.sync
            def _(e):
                emit("sp", e)
    return nc


def _fm(v, K):
    return np.ascontiguousarray(np.asarray(v, np.float32).reshape(K, 128).T)


def _units(W, KD):
    rows, ncols = W.shape
    assert rows == KD * 128
    return np.ascontiguousarray(W.reshape(KD, 128, ncols // 128, 128).transpose(2, 1, 0, 3)).reshape(ncols // 128, 128, KD * 128)


def prepare(cfg, inp):
    D, KD, KF, KH, depth = cfg.D, cfg.KD, cfg.KF, cfg.KH, cfg.depth
    f32 = np.float32
    units = []
    pps = np.zeros((depth, 128, cfg.ppw), f32)
    for i in range(depth):
        j = i // 2
        pr = pps[i]
        pr[:, cfg.o_mpre:cfg.o_mpre + KD] = _fm(inp["norm_mix_pre"][i], KD)
        pr[:, cfg.o_mpost:cfg.o_mpost + KD] = _fm(inp["norm_mix_post"][i], KD)
        pr[:, cfg.o_fpre:cfg.o_fpre + KD] = _fm(inp["norm_ffn_pre"][i], KD)
        pr[:, cfg.o_fpost:cfg.o_fpost + KD] = _fm(inp["norm_ffn_post"][i], KD)
        fw = np.asarray(inp["f_w_dw"][i], f32)
        pr[:, cfg.o_fw:cfg.o_fw + KF * CF] = fw.reshape(CF, KF, 128).transpose(2, 1, 0).reshape(128, KF * CF)
        pr[:, cfg.o_fb:cfg.o_fb + KF] = _fm(inp["f_b_dw"][i], KF)
        om = cfg.o_mix
        if i % 2 == 0:
            w_in = np.asarray(inp["a_w_in"][j], f32)
            u_in = _units(w_in, KD)
            units.append(u_in[KD:2 * KD])
            units.append(u_in[0:KD])
            units.append(_units(np.asarray(inp["a_w_out"][j], f32), KD))
            pr[:, om:om + 2 * KD] = _fm(inp["a_b_in"][j], 2 * KD)
            pr[:, om + 2 * KD:om + 3 * KD] = _fm(inp["a_ln_g"][j], KD)
            pr[:, om + 3 * KD:om + 4 * KD] = _fm(inp["a_ln_b"][j], KD)
        else:
            w_in = np.asarray(inp["b_w_in"][j], f32)
            u_in = _units(w_in, KD)
            inter = np.empty((2 * KD,) + u_in.shape[1:], f32)
            inter[0::2] = u_in[0:KD]
            inter[1::2] = u_in[KD:2 * KD]
            units.append(inter)
            units.append(_units(np.asarray(inp["b_w_out"][j], f32), KD))
            pr[:, om:om + 2 * KD] = _fm(inp["b_b_in"][j], 2 * KD)
            wd = np.asarray(inp["b_w_dw"][j], f32)
            pr[:, om + 2 * KD:om + 2 * KD + CB * KD] = wd.reshape(CB, KD, 128).transpose(2, 1, 0).reshape(128, KD * CB)
            o_b = om + 2 * KD + CB * KD
            pr[:, o_b:o_b + KD] = _fm(inp["b_b_dw"][j], KD)
            pr[:, o_b + KD:o_b + 2 * KD] = _fm(inp["b_ln_g"][j], KD)
            pr[:, o_b + 2 * KD:o_b + 3 * KD] = _fm(inp["b_ln_b"][j], KD)
        u_up = _units(np.asarray(inp["f_w_up"][i], f32), KD)
        inter = np.empty_like(u_up)
        inter[0::2] = u_up[0:KH]
        inter[1::2] = u_up[KH:2 * KH]
        units.append(inter)
        wdn = np.asarray(inp["f_w_down"][i], f32)
        d0 = _units(wdn[0:D], KD)
        d1 = _units(wdn[D:2 * D], KD)
        inter = np.empty((2 * KD,) + d0.shape[1:], f32)
        inter[0::2] = d0
        inter[1::2] = d1
        units.append(inter)
    ws = np.concatenate(units, axis=0)
    assert ws.shape[0] == cfg.NU, (ws.shape, cfg.NU)
    ws = ws.reshape(cfg.NU * 128, KD * 128)
    wst = np.ascontiguousarray(np.asarray(inp["a_w_s"], f32).transpose(0, 3, 1, 2)).reshape(cfg.NA * 128, GA * 128)
    bs = np.ascontiguousarray(np.asarray(inp["a_b_s"], f32)).reshape(cfg.NA, GA * 128)
    shared = dict(ws=ws, pp=pps.reshape(depth * 128, cfg.ppw), wst=wst, bs=bs,
                  ident=np.eye(128, dtype=f32))
    x_prompt = np.asarray(inp["x_prompt"], f32)
    x_sample = np.asarray(inp["x_sample"], f32)
    sc = np.asarray(inp["state_conv"], f32)
    sf = np.asarray(inp["state_ffn"], f32)
    per_seq = x_prompt.shape[1] // SHARD
    in_maps = []
    for c in range(NCORES):
        b, q = divmod(c, per_seq)
        s0 = q * SHARD
        xpc = np.zeros((HALO + SHARD, D), f32)
        lo = max(0, s0 - HALO)
        xpc[HALO - (s0 - lo):] = x_prompt[b, lo:s0 + SHARD]
        m = dict(shared)
        m["xp"] = xpc
        m["xs"] = np.ascontiguousarray(x_sample[c])
        m["valid"] = np.full((128, 1), 0.0 if q == 0 else 1.0, f32)
        m["sconv"] = np.ascontiguousarray(sc[:, c]).reshape(cfg.NB * (CB - 1), D)
        m["sffn"] = np.ascontiguousarray(sf[:, c]).reshape(depth * (CF - 1), 4 * D)
        in_maps.append(m)
    return in_maps


def assemble(cfg, res, batch, seq):
    D, depth = cfg.D, cfg.depth
    per_seq = seq // SHARD
    y_prompt = np.empty((batch, seq, D), np.float32)
    for c in range(NCORES):
        b, q = divmod(c, per_seq)
        y_prompt[b, q * SHARD:(q + 1) * SHARD] = res[c]["yp"]
    y_sample = np.stack([res[c]["ys"] for c in range(NCORES)])
    last = [b * per_seq + per_seq - 1 for b in range(batch)]
    ncp = np.stack([res[c]["ncp"].reshape(cfg.NB, CB - 1, D) for c in last], axis=1)
    nfp = np.stack([res[c]["nfp"].reshape(depth, CF - 1, 4 * D) for c in last], axis=1)
    ncs = np.stack([res[c]["ncs"].reshape(cfg.NB, CB - 1, D) for c in range(NCORES)], axis=1)
    nfs = np.stack([res[c]["nfs"].reshape(depth, CF - 1, 4 * D) for c in range(NCORES)], axis=1)
    nvs = np.stack([res[c]["nvs"].reshape(cfg.NA, NS, D) for c in range(NCORES)], axis=1)
    return (y_prompt, y_sample, ncp, nfp, ncs, nfs, nvs)


_CACHE = {}


def run(cfg, inputs):
    key = (cfg.D, cfg.depth)
    if key not in _CACHE:
        _CACHE[key] = build(cfg)
    nc = _CACHE[key]
    in_maps = prepare(cfg, inputs)
    res = run_bass_kernel_spmd(nc, in_maps, core_ids=list(range(NCORES)))
    xpr = np.asarray(inputs["x_prompt"])
    return assemble(cfg, res.results, xpr.shape[0], xpr.shape[1])


def kernel(**inputs):
    return run(Cfg(4096, 4), inputs)
```
